# Optimizing a Trainium2 kernel written in Bass

```python
import math
import jax, jax.numpy as jnp
from jax import lax
import numpy as np

D_MODEL = 1024
BATCH = 8
SEQ = 8192
DEPTH = 2

GRID_W = 64
CTX_LEN = 256
N_HEADS = 16
HEAD_DIM = D_MODEL // N_HEADS
WIN_R_MAX = 8
WIN_C = 16
CONV_W = 3
N_MIXERS = 2
N_CONV_LAYERS = (DEPTH + 1) // 2
N_NA_LAYERS = DEPTH // 2
EPS = 1e-6

kernel_name = "hybrid_conv_natten_prefix_dit"


def rms_norm(x, g):
    xf = x.astype(jnp.float32)
    y = xf * lax.rsqrt(jnp.mean(xf * xf, axis=-1, keepdims=True) + EPS)
    return y.astype(x.dtype) * g


def modulation(cvec, w, b):
    m = jax.nn.silu(cvec) @ w + b
    return jnp.split(m, 3, axis=-1)


def dwconv3_centred(u, w):
    up = jnp.pad(u, ((0, 0), (1, 1), (0, 0)))
    return w[0] * up[:, :-2] + w[1] * up[:, 1:-1] + w[2] * up[:, 2:]


def conv_mixer(h, w_in, conv_w, w_out):
    b_g, c_g, u, z = jnp.split(h @ w_in, 4, axis=-1)
    y = b_g * dwconv3_centred(c_g * u, conv_w) * jax.nn.silu(z)
    return y @ w_out


def dense_attention(q, k, v):
    s = jnp.einsum('bqhd,bkhd->bhqk', q, k) * (q.shape[-1] ** -0.5)
    p = jax.nn.softmax(s.astype(jnp.float32), axis=-1).astype(v.dtype)
    return jnp.einsum('bhqk,bkhd->bqhd', p, v)


def neighbourhood_attention(q, k, v, k_ctx, v_ctx, rpb):
    B, L, H, Dh = q.shape
    rows = L // GRID_W
    win_r = min(WIN_R_MAX, rows)
    n_lat = win_r * WIN_C
    qg = (q * (Dh ** -0.5)).reshape(B, rows, GRID_W, H, Dh)
    kg = k.reshape(B, rows, GRID_W, H, Dh)
    vg = v.reshape(B, rows, GRID_W, H, Dh)
    cols = jnp.arange(GRID_W)
    col_start = jnp.clip(cols - WIN_C // 2, 0, GRID_W - WIN_C)
    col_idx = col_start[:, None] + jnp.arange(WIN_C)[None, :]
    dc_idx = col_idx - cols[:, None] + (WIN_C - 1)

    def row_step(args):
        r, q_row = args
        rs = jnp.clip(r - win_r // 2, 0, rows - win_r)
        k_band = lax.dynamic_slice_in_dim(kg, rs, win_r, axis=1)
        v_band = lax.dynamic_slice_in_dim(vg, rs, win_r, axis=1)
        k_win = jnp.take(k_band, col_idx, axis=2)
        v_win = jnp.take(v_band, col_idx, axis=2)
        dr_idx = rs + jnp.arange(win_r) - r + (WIN_R_MAX - 1)
        bias = rpb[:, dr_idx[:, None, None], dc_idx[None, :, :]]
        s_lat = jnp.einsum('bchd,bvcwhd->bhcvw', q_row, k_win) + jnp.transpose(bias, (0, 2, 1, 3))[None]
        s_ctx = jnp.einsum('bchd,bjhd->bhcj', q_row, k_ctx)
        logits = jnp.concatenate([s_lat.reshape(B, H, GRID_W, n_lat), s_ctx], axis=-1)
        p = jax.nn.softmax(logits.astype(jnp.float32), axis=-1).astype(v.dtype)
        p_lat = p[..., :n_lat].reshape(B, H, GRID_W, win_r, WIN_C)
        p_ctx = p[..., n_lat:]
        return (jnp.einsum('bhcvw,bvcwhd->bchd', p_lat, v_win)
                + jnp.einsum('bhcj,bjhd->bchd', p_ctx, v_ctx))

    out = lax.map(row_step, (jnp.arange(rows), jnp.moveaxis(qg, 1, 0)))
    return jnp.moveaxis(out, 0, 1).reshape(B, L, H, Dh)


def setup_inputs(seed: int = 0) -> dict:
    key = jax.random.key(seed)
    ks = jax.random.split(key, 16)
    D = D_MODEL
    s = D ** -0.5
    return {
        "x": jax.random.normal(ks[0], (BATCH, SEQ, D), jnp.float32),
        "c": jax.random.normal(ks[1], (BATCH, D), jnp.float32),
        "ctx": jax.random.normal(ks[2], (BATCH, CTX_LEN, D), jnp.float32),
        "c_ctx": jax.random.normal(ks[3], (D,), jnp.float32),
        "g_pre": 1.0 + 0.1 * jax.random.normal(ks[4], (DEPTH, D), jnp.float32),
        "g_post": 1.0 + 0.1 * jax.random.normal(ks[5], (DEPTH, D), jnp.float32),
        "w_mod": 0.5 * s * jax.random.normal(ks[6], (DEPTH, D, 3 * D), jnp.float32),
        "b_mod": 0.02 * jax.random.normal(ks[7], (DEPTH, 3 * D), jnp.float32),
        "w_in_conv": s * jax.random.normal(ks[8], (N_CONV_LAYERS, D, 4 * D), jnp.float32),
        "conv_w": (CONV_W ** -0.5) * jax.random.normal(ks[9], (N_CONV_LAYERS, CONV_W, D), jnp.float32),
        "w_out_conv": s * jax.random.normal(ks[10], (N_CONV_LAYERS, D, D), jnp.float32),
        "w_in_na": s * jax.random.normal(ks[11], (N_NA_LAYERS, D, 4 * D), jnp.float32),
        "rpb": 0.1 * jax.random.normal(ks[12], (N_NA_LAYERS, N_HEADS, 2 * WIN_R_MAX - 1, 2 * WIN_C - 1), jnp.float32),
        "w_out_na": s * jax.random.normal(ks[13], (N_NA_LAYERS, D, D), jnp.float32),
    }


def reference(x, c, ctx, c_ctx, g_pre, g_post, w_mod, b_mod,
              w_in_conv, conv_w, w_out_conv, w_in_na, rpb, w_out_na):
    B, L, D = x.shape
    Bc, Lc, _ = ctx.shape
    for i in range(DEPTH):
        last = i == DEPTH - 1
        j = i // N_MIXERS
        sh, sc, gt = modulation(c, w_mod[i], b_mod[i])
        sh_c, sc_c, gt_c = modulation(c_ctx, w_mod[i], b_mod[i])
        h = rms_norm(x, g_pre[i]) * (1 + sc[:, None, :]) + sh[:, None, :]
        hc = rms_norm(ctx, g_pre[i]) * (1 + sc_c) + sh_c
        if i % N_MIXERS == 0:
            y = conv_mixer(h, w_in_conv[j], conv_w[j], w_out_conv[j])
            if not last:
                yc = conv_mixer(hc, w_in_conv[j], conv_w[j], w_out_conv[j])
        else:
            wi = w_in_na[j]
            q, k, v, z = jnp.split(h @ wi, 4, axis=-1)
            k_c, v_c = jnp.split(hc @ wi[:, D:3 * D], 2, axis=-1)
            heads = lambda t, n: t.reshape(B, n, N_HEADS, HEAD_DIM)
            k_c, v_c = heads(k_c, Lc), heads(v_c, Lc)
            o = neighbourhood_attention(heads(q, L), heads(k, L), heads(v, L), k_c, v_c, rpb[j])
            y = (jax.nn.silu(z) * o.reshape(B, L, D)) @ w_out_na[j]
            if not last:
                q_c = heads(hc @ wi[:, :D], Lc)
                z_c = hc @ wi[:, 3 * D:]
                oc = dense_attention(q_c, k_c, v_c)
                yc = (jax.nn.silu(z_c) * oc.reshape(Bc, Lc, D)) @ w_out_na[j]
        x = x + gt[:, None, :] * rms_norm(y, g_post[i])
        if not last:
            ctx = ctx + gt_c * rms_norm(yc, g_post[i])
    return x
```

```python
import os
import numpy as np
from contextlib import ExitStack
import concourse.bass as bass
import concourse.mybir as mybir
from concourse.bass_utils import run_bass_kernel_spmd

F32 = mybir.dt.float32
BF16 = mybir.dt.bfloat16
AF = mybir.ActivationFunctionType
ALU = mybir.AluOpType

D = 1024
L = 8192
LC = 256
NCORES = 8
EPS = 1e-6
KC = 8
NB = L // 512


class Buf:
    __slots__ = ("w", "r", "name")

    def __init__(self, name=""):
        self.w = None
        self.r = {}
        self.name = name


class Prog:
    ENG = ("pe", "act", "dve", "pool", "sp")

    def __init__(self, nc, es):
        self.nc = nc
        self.es = es
        self.q = {e: [] for e in self.ENG}
        self.cnt = {}
        self.sems = {}
        self.seen = {e: {} for e in self.ENG}
        for e in ("pe", "act", "dve", "pool"):
            self._sem("c_" + e)

    def _sem(self, key):
        if key not in self.sems:
            self.sems[key] = self.es.enter_context(self.nc.semaphore(key))
            self.cnt[key] = 0
        return self.sems[key]

    def _deps(self, eng, reads, writes):
        deps = []
        for b in reads:
            if b.w is not None:
                deps.append(b.w)
        for b in writes:
            if b.w is not None and b.w[2] != eng:
                deps.append(b.w)
            for t in b.r.values():
                if t[2] != eng:
                    deps.append(t)
        for (k, v, src) in deps:
            if src == "pe" and eng == "pe":
                continue
            if self.seen[eng].get(k, 0) >= v:
                continue
            self.seen[eng][k] = v
            self.q[eng].append(("wait", k, v))

    def _commit(self, tok, reads, writes):
        for b in reads:
            old = b.r.get(tok[0])
            if old is None or old[1] < tok[1]:
                b.r[tok[0]] = tok
        for b in writes:
            b.w = tok
            b.r = {}

    def op(self, eng, fn, reads=(), writes=()):
        self._deps(eng, reads, writes)
        key = "c_" + eng
        self.cnt[key] += 1
        tok = (key, self.cnt[key], eng)
        self.q[eng].append(("op", fn, key, 1))
        self._commit(tok, reads, writes)
        return tok

    def dma(self, qeng, semkey, out, in_, reads=(), writes=()):
        self._sem(semkey)
        self._deps(qeng, reads, writes)
        self.cnt[semkey] += 16
        tok = (semkey, self.cnt[semkey], "dma")
        self.q[qeng].append(("op", lambda e: e.dma_start(out=out, in_=in_), semkey, 16))
        self._commit(tok, reads, writes)
        return tok

    def barrier(self):
        for e in self.ENG:
            for k, v in self.cnt.items():
                if v > 0 and self.seen[e].get(k, 0) < v:
                    self.seen[e][k] = v
                    self.q[e].append(("wait", k, v))

    def wait_all(self, eng, bufs):
        self._deps(eng, bufs, ())

    def emit(self, block):
        def run(e, items):
            for it in items:
                if it[0] == "wait":
                    e.wait_ge(self.sems[it[1]], it[2])
                else:
                    ins = it[1](e)
                    ins.then_inc(self.sems[it[2]], it[3])

        block.tensor(lambda e: run(e, self.q["pe"]))
        block.scalar(lambda e: run(e, self.q["act"]))
        block.vector(lambda e: run(e, self.q["dve"]))
        block.gpsimd(lambda e: run(e, self.q["pool"]))
        block.sync(lambda e: run(e, self.q["sp"]))


def build(stop_after="all"):
    nc = bass.Bass("TRN2", target_bir_lowering=False)
    es = ExitStack()
    with es:
        _build(nc, es, stop_after)
    return nc


def _build(nc, es, stop_after):
    P = Prog(nc, es)

    def din(name, shape, dt=F32):
        return nc.dram_tensor(name, list(shape), dt, kind="ExternalInput").ap()

    x_d = din("x", [L, D])
    ctx_d = din("ctx", [LC, D])
    cT_d = din("cT", [128, 16])
    wmod_d = din("w_mod", [2, D, 3 * D])
    bmodT_d = din("bmodT", [128, 48])
    bgate_d = din("bgate", [2, D])
    gpreT_d = din("gpreT", [128, 16])
    gpost_d = din("gpost", [2, D])
    winc_d = din("w_in_conv", [D, 4 * D])
    woutc_d = din("w_out_conv", [D, D])
    winn_d = din("w_in_na", [D, 4 * D])
    woutn_d = din("w_out_na", [D, D])
    convwT_d = din("convwT", [128, 24])
    out_d = nc.dram_tensor("out", [L, D], F32, kind="ExternalOutput").ap()
    x1_d = nc.dram_tensor("x1_scr", [L, D], F32).ap()
    ctx1_d = nc.dram_tensor("ctx1_scr", [LC, D], F32).ap()
    qT_d = nc.dram_tensor("qT_scr", [NB, 128, 4 * KC * 128], BF16).ap()
    kT_d = nc.dram_tensor("kT_scr", [NB, 128, 4 * KC * 128], BF16).ap()
    szT_d = nc.dram_tensor("szT_scr", [NB, 128, 4 * KC * 128], BF16).ap()
    v_d = nc.dram_tensor("v_scr", [L, D], BF16).ap()
    rpbG_d = din("rpbG", [128, 16 * 24 * 64])

    def sb(name, shape, dt=F32):
        return es.enter_context(nc.sbuf_tensor("sb_" + name, list(shape), dt))

    def ps(name, shape, dt=F32):
        return es.enter_context(nc.psum_tensor("ps_" + name, list(shape), dt))

    ident = sb("ident", [128, 128], BF16)
    ones1 = sb("ones1", [1, 128], F32)
    cT = sb("cT", [128, 16])
    siluT = sb("siluT", [128, 16])
    bmodT = sb("bmodT_sb", [128, 48])
    gpreT = sb("gpreT_sb", [128, 16])
    convw = sb("convw_sb", [128, 24])
    Win = sb("Win", [128, KC, 4 * D], BF16)
    Wout = sb("Wout", [128, KC, D], BF16)
    Win_b, Wout_b = Buf("Win"), Buf("Wout")
    ones64 = sb("ones64", [128, 64], BF16)
    cst = sb("cst", [128, 2])
    kcT_b, vc_b = Buf("kcT"), Buf("vc")
    modT = sb("modT", [128, 2, 16, 2])
    Gs = sb("Gs", [128, 2, 2, 8])
    SHs = sb("SHs", [128, 2, 2, 8])
    GT = sb("GT", [128, 3, D])
    es_mod = ExitStack()

    def sbm(name, shape, dt=F32):
        return es_mod.enter_context(nc.sbuf_tensor("sm_" + name, list(shape), dt))

    identf = sbm("identf", [128, 128], F32)
    silu_bc = sbm("silu_bc", [128, 16, 128])
    bgate = sbm("bgate_sb", [1, 2 * D])
    gpost_bc = sbm("gpost_bc", [128, 2, D])

    b_const = Buf("const")

    psS = ps("psS", [128, 4, 512])
    pp4 = ps("pp4", [128, 512])
    pp = [psS[:, i, :] for i in range(4)] + [pp4[:, :]]
    pp_b = [Buf(f"pp{i}") for i in range(5)]
    py = ps("py", [128, 2, 512])
    py_b = [Buf("py0"), Buf("py1")]
    ptp = ps("ptp", [128, 8, 128], BF16)
    ptp_b = Buf("ptp")
    pp_rr = [0]

    def next_pp():
        i = pp_rr[0] % 4
        pp_rr[0] += 1
        return pp[i], pp_b[i]

    ptps = [ptp[:], pp4[:].bitcast(BF16).rearrange("p (k n) -> p k n", k=KC)]
    ptps_b = [ptp_b, pp_b[4]]
    tp_rr = [0]

    def mk_ident(e):
        e.memset(ones64[:], 1.0)
        e.memset(cst[:, 0:1], float(EPS))
        e.memset(cst[:, 1:2], -0.5)
        return e.memset(ones1[:], 1.0)

    P.op("pool", mk_ident, writes=[b_const])
    ld_b = Buf("ld_small")
    P.dma("sp", "d_small", cT[:], cT_d, writes=[ld_b])
    P.dma("sp", "d_small", bmodT[:], bmodT_d, writes=[ld_b])
    P.dma("sp", "d_small", gpreT[:], gpreT_d, writes=[ld_b])
    P.dma("sp", "d_small", convw[:], convwT_d, writes=[ld_b])
    P.dma("sp", "d_small", bgate[:], bgate_d.rearrange("(o a) d -> o (a d)", o=1), writes=[ld_b])
    P.dma("sp", "d_small", gpost_bc[:].rearrange("p a d -> p (a d)"),
          gpost_d.rearrange("(o a) d -> o (a d)", o=1).partition_broadcast(128), writes=[ld_b])
    identd = din("identd", [128, 128])
    P.dma("sp", "d_small", identf[:], identd, reads=[b_const], writes=[ld_b])

    siluT_b = Buf("siluT")
    P.op("act", lambda e: e.activation(out=siluT[:], in_=cT[:], func=AF.Silu), reads=[ld_b], writes=[siluT_b])
    ident_b = Buf("ident")
    P.op("dve", lambda e: e.tensor_copy(out=ident[:], in_=identf[:]), reads=[ld_b], writes=[ident_b])
    silu_bc_b = Buf("silu_bc")
    P.op("dve", lambda e: e.tensor_copy(out=silu_bc[:], in_=siluT[:].unsqueeze(2).to_broadcast([128, 16, 128])),
         reads=[siluT_b], writes=[silu_bc_b])

    NWST = 4
    wst = [sbm(f"wst{i}", [128, KC, 512], F32) for i in range(NWST)]
    wst_b = [Buf(f"wst{i}") for i in range(NWST)]
    modT_b = Buf("modT")
    GT_b = Buf("GT")
    cvt_rr = [0]
    ci_holder = [0]
    wtag = [0]

    def weight_chunks(dst, dst_b, src_d, ncols, stg, stg_b, wcol):
        out = []
        wtag[0] += 1
        tag = wtag[0]
        for c in range(ncols // wcol):
            def emit(c=c):
                s = ci_holder[0] % len(stg)
                ci_holder[0] += 1
                P.dma(("sp", "act")[ci_holder[0] % 2], f"d_wl{tag % 2}_{s}", stg[s],
                      src_d[:, c * wcol:(c + 1) * wcol].rearrange("(k p) n -> p k n", p=128), writes=[stg_b[s]])
                eng = ("dve", "act")[cvt_rr[0] % 2]
                cvt_rr[0] += 1
                if eng == "act":
                    P.op("act", lambda e: e.copy(out=dst[:, :, c * wcol:(c + 1) * wcol], in_=stg[s]),
                         reads=[stg_b[s]], writes=[dst_b])
                else:
                    P.op("dve", lambda e: e.tensor_copy(out=dst[:, :, c * wcol:(c + 1) * wcol], in_=stg[s]),
                         reads=[stg_b[s]], writes=[dst_b])
            out.append(emit)
        return out

    def load_weight_bf16(dst, dst_b, src_d, ncols, stg, stg_b, wcol):
        for f_ in weight_chunks(dst, dst_b, src_d, ncols, stg, stg_b, wcol):
            f_()

    stgA = [wst[2][:], wst[3][:]]
    stgA_b = [wst_b[2], wst_b[3]]
    pendingA = (weight_chunks(Win, Win_b, winc_d, 4 * D, stgA, stgA_b, 512)
                + weight_chunks(Wout, Wout_b, woutc_d, D, stgA, stgA_b, 512))
    ci = 0
    for layer in range(2):
        for nch in range(6):
            for _ in range(4):
                if pendingA:
                    pendingA.pop(0)()
            s = ci % 2
            ci += 1
            P.dma(("sp", "act")[ci % 2], f"d_wst{s}", wst[s][:],
                  wmod_d[layer, :, nch * 512:(nch + 1) * 512].rearrange("(k p) n -> p k n", p=128),
                  writes=[wst_b[s]])
            if nch < 4:
                bank, bb = next_pp()

                def mm(e, s=s, bank=bank):
                    ins = None
                    for m in range(4):
                        for k in range(KC):
                            ins = e.matmul(bank[:, m * 2:m * 2 + 2], lhsT=wst[s][:, k, m * 128:(m + 1) * 128],
                                           rhs=siluT[:].rearrange("p (w k) -> p k w", w=2)[:, k, :],
                                           start=(k == 0), stop=(k == KC - 1))
                    return ins

                P.op("pe", mm, reads=[wst_b[s], siluT_b], writes=[bb])
                P.op("dve", lambda e, bank=bank, layer=layer, nch=nch: e.tensor_copy(
                    out=modT[:, layer, nch * 4:(nch + 1) * 4, :],
                    in_=bank[:, 0:8].rearrange("p (m w) -> p m w", w=2)),
                    reads=[bb], writes=[modT_b])
            else:
                half = nch - 4
                for w in range(2):
                    if layer == 1 and w == 1:
                        continue
                    gi = 0 if (layer == 0 and w == 0) else (1 if layer == 0 else 2)
                    bank, bb = next_pp()

                    def mmg(e, s=s, bank=bank, w=w, layer=layer, half=half):
                        for k in range(KC):
                            e.matmul(bank[:, :], lhsT=silu_bc[:, w * 8 + k, :], rhs=wst[s][:, k, :],
                                     start=(k == 0), stop=False)
                        return e.matmul(bank[:, :], lhsT=ones1[:, :],
                                        rhs=bgate[:, layer * D + half * 512: layer * D + (half + 1) * 512],
                                        start=False, stop=True)

                    P.op("pe", mmg, reads=[wst_b[s], silu_bc_b, ld_b, b_const], writes=[bb])
                    P.op("dve", lambda e, bank=bank, gi=gi, layer=layer, half=half: e.scalar_tensor_tensor(
                        out=GT[:, gi, half * 512:(half + 1) * 512], in0=bank[:, :], scalar=1.0,
                        in1=gpost_bc[:, layer, half * 512:(half + 1) * 512], op0=ALU.mult, op1=ALU.mult),
                        reads=[bb, ld_b], writes=[GT_b])
    GS_b = Buf("GS")
    for layer in range(2):
        for w in range(2):
            P.op("dve", lambda e, layer=layer, w=w: e.tensor_tensor(
                out=SHs[:, layer, w, :], in0=modT[:, layer, 0:8, w], in1=bmodT[:, layer * 24:layer * 24 + 8],
                op=ALU.add), reads=[modT_b, ld_b], writes=[GS_b])
            P.op("dve", lambda e, layer=layer, w=w: e.scalar_tensor_tensor(
                out=Gs[:, layer, w, :], in0=modT[:, layer, 8:16, w], scalar=1.0,
                in1=bmodT[:, layer * 24 + 8:layer * 24 + 16], op0=ALU.add, op1=ALU.add),
                reads=[modT_b, ld_b], writes=[GS_b])
            P.op("dve", lambda e, layer=layer, w=w: e.scalar_tensor_tensor(
                out=Gs[:, layer, w, :], in0=Gs[:, layer, w, :], scalar=1.0,
                in1=gpreT[:, layer * 8:(layer + 1) * 8], op0=ALU.mult, op1=ALU.mult),
                reads=[GS_b, ld_b], writes=[GS_b])

    ci_holder = [ci]

    while pendingA:
        pendingA.pop(0)()
    P.barrier()
    es_mod.close()

    esA = ExitStack()

    def sbA(name, shape, dt=F32):
        return esA.enter_context(nc.sbuf_tensor("sa_" + name, list(shape), dt))

    NX = 2
    xin = [sbA(f"xin{i}", [128, D]) for i in range(NX)]
    xin_b = [Buf(f"xin{i}") for i in range(NX)]
    sqj = sbA("sqj", [128, D], BF16)
    ss = sbA("ss", [128, 8])
    rs = sbA("rs", [128, 8])
    st_b = [Buf(f"st{i}") for i in range(8)]
    xn = [sbA(f"xn{i}", [128, D], BF16) for i in range(4)]
    xn_b = [Buf(f"xn{i}") for i in range(4)]
    hT = [sbA(f"hT{i}", [128, KC, 512], BF16) for i in range(3)]
    hT_b = [[Buf(f"hT{i}_{t}") for t in range(4)] for i in range(3)]
    cu = [sbA(f"cu{i}", [128, KC, 514]) for i in range(2)]
    cu_b = [[Buf(f"cu{i}_{c}") for c in range(KC)] for i in range(2)]
    hl_b = [Buf("hl0"), Buf("hl1")]
    hr_b = [Buf("hr0"), Buf("hr1")]
    cg = [sbA(f"cg{i}", [128, 512]) for i in range(2)]
    cg_b = [Buf("cg0"), Buf("cg1")]
    szb = [sbA(f"sz{i}", [128, 512]) for i in range(2)]
    sz_b = [Buf("sz0"), Buf("sz1")]
    cv = [sbA(f"cv{i}", [128, 512]) for i in range(2)]
    cv_b = [Buf("cv0"), Buf("cv1")]
    yT = sbA("yT", [128, KC, 512], BF16)
    yT_b = [Buf(f"yT{c}") for c in range(KC)]
    tmp = [sbA(f"tmp{i}", [128, D]) for i in range(2)]
    tmp_b = [Buf("tmp0"), Buf("tmp1")]
    xres = [sbA(f"xres{i}", [128, D]) for i in range(NX)]
    xres_b = [Buf(f"xres{i}") for i in range(NX)]
    ss2 = sbA("ss2", [128, 8, 2])
    rs2 = sbA("rs2", [128, 8])
    st2_b = [Buf(f"st2{i}") for i in range(8)]

    cnt = {"tile": 0, "chunk": 0, "otile": 0}
    fbA = (xin, xin_b, sqj, ss, rs, st_b, xn, xn_b, hT, hT_b)

    def rstd_chain(ms_ap, rs_ap, buf):
        P.op("pool", lambda e: e.tensor_tensor(out=ms_ap, in0=ms_ap, in1=cst[:, 0:1], op=ALU.add),
             reads=[buf, b_const], writes=[buf])
        P.op("pool", lambda e: e.tensor_tensor(out=rs_ap, in0=ms_ap, in1=cst[:, 1:2], op=ALU.pow),
             reads=[buf, b_const], writes=[buf])

    NXN = 4

    def front_a(fb, src_d, tok0, ntile, src_bufs=None):
        xin, xin_b, sqj, ss, rs, st_b, xn, xn_b, hT, hT_b = fb
        slots = []
        for tt in range(ntile):
            i = cnt["tile"]
            cnt["tile"] += 1
            xs = i % NX
            si = i % 8
            xb = i % NXN
            slots.append(xb)
            P.dma("sp", f"d_xin{xs}", xin[xs][:], src_d[tok0 + tt * 128: tok0 + (tt + 1) * 128, :],
                  reads=([] if src_bufs is None else [src_bufs(tok0 + tt * 128)]), writes=[xin_b[xs]])
            P.op("act", lambda e, xs=xs, si=si: e.activation(out=sqj[:], in_=xin[xs][:], func=AF.Square,
                                                            scale=1.0 / 32.0, accum_out=ss[:, si:si + 1]),
                 reads=[xin_b[xs]], writes=[st_b[si]])
            rstd_chain(ss[:, si:si + 1], rs[:, si:si + 1], st_b[si])
            P.op("act", lambda e, xs=xs, si=si, xb=xb: e.activation(out=xn[xb][:], in_=xin[xs][:], func=AF.Copy,
                                                                   scale=rs[:, si:si + 1]),
                 reads=[xin_b[xs], st_b[si]], writes=[xn_b[xb]])
        return slots

    def front_b(fb, slots, slot, layer, w, only=None):
        xin, xin_b, sqj, ss, rs, st_b, xn, xn_b, hT, hT_b = fb
        for tt, xb in enumerate(slots):
            if only is not None and tt != only:
                continue
            pi = tp_rr[0] % 2
            tp_rr[0] += 1
            tpb = ptps[pi]
            tpb_b = ptps_b[pi]

            def tps(e, xb=xb, tpb=tpb):
                ins = None
                for k in range(KC):
                    ins = e.transpose(tpb[:, k, :], xn[xb][:, k * 128:(k + 1) * 128], ident[:])
                return ins

            P.op("pe", tps, reads=[xn_b[xb], ident_b], writes=[tpb_b])
            for k in range(KC):
                P.op("dve", lambda e, k=k, tt=tt, tpb=tpb: e.tensor_scalar(
                    out=hT[slot][:, k, tt * 128:(tt + 1) * 128], in0=tpb[:, k, :],
                    scalar1=Gs[:, layer, w, k:k + 1], scalar2=SHs[:, layer, w, k:k + 1],
                    op0=ALU.mult, op1=ALU.add),
                    reads=[tpb_b, GS_b], writes=[hT_b[slot][tt]])

    def proj(hT_t, hT_bl, nt, col0):
        bank, bb = next_pp()

        def mm(e):
            ins = None
            for k in range(KC):
                ins = e.matmul(bank[:, 0:nt], lhsT=Win[:, k, col0:col0 + 128], rhs=hT_t[:, k, 0:nt],
                               start=(k == 0), stop=(k == KC - 1))
            return ins

        P.op("pe", mm, reads=[Win_b] + list(hT_bl), writes=[bb])
        return bank, bb

    blocks = [(ctx_d, ctx1_d, 0, 2, 1, 1, True, True)]
    for j in range(NB):
        blocks.append((x_d, x1_d, j * 512, 4, 0, 0, j == 0, j == NB - 1))
    if stop_after in ("A_small", "small", "AB_small"):
        blocks = blocks[:3]
        blocks[-1] = blocks[-1][:7] + (True,)

    x1_b = {}
    def dbuf(key):
        if key not in x1_b:
            x1_b[key] = Buf(str(key))
        return x1_b[key]

    def mm1(bi, hooks=None):
        src, dst, tok0, ntile, w, gi, first, last = blocks[bi]
        slot = bi % 2
        hs = bi % 3
        nt = ntile * 128
        for oc in range(KC):
            i = cnt["chunk"]
            cnt["chunk"] += 1
            cs = i % 2
            if hooks and oc in hooks:
                for f_ in hooks[oc]:
                    f_()
            bank_c, bb_c = proj(hT[hs], hT_b[hs][:ntile], nt, D + oc * 128)
            bank_u, bb_u = proj(hT[hs], hT_b[hs][:ntile], nt, 2 * D + oc * 128)
            P.op("act", lambda e, bank_c=bank_c, cs=cs: e.copy(out=cg[cs][:, 0:nt], in_=bank_c[:, 0:nt]),
                 reads=[bb_c], writes=[cg_b[cs]])
            P.op("dve", lambda e, bank_u=bank_u, cs=cs, oc=oc: e.tensor_tensor(
                out=cu[slot][:, oc, 1:nt + 1], in0=bank_u[:, 0:nt], in1=cg[cs][:, 0:nt], op=ALU.mult),
                reads=[bb_u, cg_b[cs]], writes=[cu_b[slot][oc]])
        if first:
            P.op("pool", lambda e: e.memset(cu[slot][:, :, 0:1], 0.0), writes=[hl_b[slot]])
        else:
            ps_ = (bi - 1) % 2
            pnt = blocks[bi - 1][3] * 128
            P.op("pool", lambda e: e.tensor_copy(out=cu[slot][:, :, 0:1], in_=cu[ps_][:, :, pnt:pnt + 1]),
                 reads=cu_b[ps_], writes=[hl_b[slot]])
            P.op("pool", lambda e: e.tensor_copy(out=cu[ps_][:, :, pnt + 1:pnt + 2], in_=cu[slot][:, :, 1:2]),
                 reads=cu_b[slot], writes=[hr_b[ps_]])
        if last:
            P.op("pool", lambda e: e.memset(cu[slot][:, :, nt + 1:nt + 2], 0.0), writes=[hr_b[slot]])

    def back(bi):
        src, dst, tok0, ntile, w, gi, first, last = blocks[bi]
        slot = bi % 2
        hs = bi % 3
        nt = ntile * 128
        for oc in range(KC):
            i = cnt["chunk"]
            cnt["chunk"] += 1
            cs = i % 2
            P.op("act", lambda e, cs=cs, oc=oc: e.activation(
                out=cv[cs][:, 0:nt], in_=cu[slot][:, oc, 1:nt + 1], func=AF.Copy,
                scale=convw[:, oc * 3 + 1:oc * 3 + 2]),
                reads=[cu_b[slot][oc], ld_b], writes=[cv_b[cs]])
            P.op("dve", lambda e, cs=cs, oc=oc: e.scalar_tensor_tensor(
                out=cv[cs][:, 0:nt], in0=cu[slot][:, oc, 0:nt], scalar=convw[:, oc * 3:oc * 3 + 1],
                in1=cv[cs][:, 0:nt], op0=ALU.mult, op1=ALU.add),
                reads=[cu_b[slot][oc], hl_b[slot], cv_b[cs]], writes=[cv_b[cs]])
            P.op("dve", lambda e, cs=cs, oc=oc: e.scalar_tensor_tensor(
                out=cv[cs][:, 0:nt], in0=cu[slot][:, oc, 2:nt + 2], scalar=convw[:, oc * 3 + 2:oc * 3 + 3],
                in1=cv[cs][:, 0:nt], op0=ALU.mult, op1=ALU.add),
                reads=[cu_b[slot][oc], hr_b[slot], cv_b[cs]], writes=[cv_b[cs]])
            bank_b, bb_b = proj(hT[hs], hT_b[hs][:ntile], nt, oc * 128)
            bank_z, bb_z = proj(hT[hs], hT_b[hs][:ntile], nt, 3 * D + oc * 128)
            P.op("act", lambda e, bank_z=bank_z, cs=cs: e.activation(out=szb[cs][:, 0:nt], in_=bank_z[:, 0:nt],
                                                                      func=AF.Silu),
                 reads=[bb_z], writes=[sz_b[cs]])
            P.op("dve", lambda e, bank_b=bank_b, cs=cs: e.tensor_tensor(
                out=cv[cs][:, 0:nt], in0=bank_b[:, 0:nt], in1=cv[cs][:, 0:nt], op=ALU.mult),
                reads=[bb_b, cv_b[cs]], writes=[cv_b[cs]])
            P.op("dve", lambda e, cs=cs, oc=oc: e.tensor_tensor(
                out=yT[:, oc, 0:nt], in0=cv[cs][:, 0:nt], in1=szb[cs][:, 0:nt], op=ALU.mult),
                reads=[cv_b[cs], sz_b[cs]], writes=[yT_b[oc]])

    def outproj_tile(bi, tt):
        src, dst, tok0, ntile, w, gi, first, last = blocks[bi]
        if tt >= ntile:
            return
        if True:
            i = cnt["otile"]
            cnt["otile"] += 1
            xs = i % NX
            si = i % 8
            ts_ = i % 2
            P.dma("sp", f"d_xres{xs}", xres[xs][:], src[tok0 + tt * 128: tok0 + (tt + 1) * 128, :],
                  writes=[xres_b[xs]])
            for h in range(2):
                def mmo(e, h=h, tt=tt):
                    ins = None
                    for k in range(KC):
                        ins = e.matmul(py[:, h, :], lhsT=yT[:, k, tt * 128:(tt + 1) * 128],
                                       rhs=Wout[:, k, h * 512:(h + 1) * 512], start=(k == 0), stop=(k == KC - 1))
                    return ins

                P.op("pe", mmo, reads=[Wout_b] + yT_b, writes=[py_b[h]])
                P.op("act", lambda e, h=h, si=si: e.activation(out=sqj[:, 0:512], in_=py[:, h, :], func=AF.Square,
                                                               scale=1.0 / 32.0, accum_out=ss2[:, si, h:h + 1]),
                     reads=[py_b[h]], writes=[st2_b[si]])
            P.op("dve", lambda e, si=si: e.tensor_tensor(out=ss2[:, si, 0:1], in0=ss2[:, si, 0:1],
                                                         in1=ss2[:, si, 1:2], op=ALU.add),
                 reads=[st2_b[si]], writes=[st2_b[si]])
            rstd_chain(ss2[:, si, 0:1], rs2[:, si:si + 1], st2_b[si])
            for h in range(2):
                P.op("dve", lambda e, h=h, si=si, ts_=ts_: e.scalar_tensor_tensor(
                    out=tmp[ts_][:, h * 512:(h + 1) * 512], in0=py[:, h, :], scalar=rs2[:, si:si + 1],
                    in1=GT[:, gi, h * 512:(h + 1) * 512], op0=ALU.mult, op1=ALU.mult),
                    reads=[py_b[h], st2_b[si], GT_b], writes=[tmp_b[ts_]])
            for h in range(2):
                P.op("dve", lambda e, ts_=ts_, xs=xs, h=h: e.tensor_tensor(
                    out=tmp[ts_][:, h * 512:(h + 1) * 512], in0=tmp[ts_][:, h * 512:(h + 1) * 512],
                    in1=xres[xs][:, h * 512:(h + 1) * 512], op=ALU.add),
                    reads=[tmp_b[ts_], xres_b[xs]], writes=[tmp_b[ts_]])
            P.dma("sp", f"d_st{ts_}", dst[tok0 + tt * 128: tok0 + (tt + 1) * 128, :], tmp[ts_][:],
                  reads=[tmp_b[ts_]], writes=[dbuf((id(dst), tok0 + tt * 128))])

    nbl = len(blocks)
    xslots = {}

    def fa(b):
        if b < nbl:
            xslots[b] = front_a(fbA, blocks[b][0], blocks[b][2], blocks[b][3])

    def fbt(b, only=None):
        if b < nbl:
            front_b(fbA, xslots[b], b % 3, 0, blocks[b][4], only=only)

    fa(0)
    fbt(0)
    fa(1)
    mm1(0)
    fbt(1)
    fa(2)
    for i in range(nbl + 1):
        if i >= 1:
            back(i - 1)
        if i + 1 < nbl:
            hooks = {}
            if i >= 1:
                for k_, oc_ in enumerate((1, 3, 5, 7)):
                    hooks.setdefault(oc_, []).append(lambda k_=k_, i=i: outproj_tile(i - 1, k_))
            for tt_, oc_ in enumerate((2, 3, 4, 5)):
                hooks.setdefault(oc_, []).append(lambda i=i, tt_=tt_: fbt(i + 2, only=tt_))
            hooks.setdefault(6, []).append(lambda i=i: fa(i + 3))
            mm1(i + 1, hooks)
        elif i >= 1:
            for k_ in range(4):
                outproj_tile(i - 1, k_)

    if stop_after in ("A", "A_small"):
        nblk = len(blocks) - 1
        for t in range(nblk * 4):
            s = t % 2
            P.dma("sp", f"d_dbg{s}", tmp[s][:], x1_d[t * 128:(t + 1) * 128, :],
                  reads=[dbuf((id(x1_d), t * 128))], writes=[tmp_b[s]])
            P.dma("sp", f"d_dbo{s}", out_d[t * 128:(t + 1) * 128, :], tmp[s][:], reads=[tmp_b[s]],
                  writes=[dbuf(("out", t))])
        P.wait_all("sp", [dbuf(("out", t)) for t in range(nblk * 4)])
        with nc.Block() as block:
            P.emit(block)
        esA.close()
        return

    P.barrier()
    esA.close()

    esBC = ExitStack()
    kcT = esBC.enter_context(nc.sbuf_tensor("sbBC_kcT", [128, KC, LC], BF16))
    vc = esBC.enter_context(nc.sbuf_tensor("sbBC_vc", [128, 2, D], BF16))
    esB = ExitStack()

    def sbB(name, shape, dt=F32):
        return esB.enter_context(nc.sbuf_tensor("sbB_" + name, list(shape), dt))

    xinB = [sbB(f"xin{i}", [128, D]) for i in range(2)]
    xinB_b = [Buf(), Buf()]
    sqjB = sbB("sqj", [128, D], BF16)
    ssB = sbB("ss", [128, 8])
    rsB = sbB("rs", [128, 8])
    stB_b = [Buf() for i in range(8)]
    xnB = [sbB(f"xn{i}", [128, D], BF16) for i in range(4)]
    xnB_b = [Buf() for i in range(4)]
    hTB = [sbB(f"hT{i}", [128, KC, 512], BF16) for i in range(3)]
    hTB_b = [[Buf() for t in range(4)] for i in range(3)]
    fbB = (xinB, xinB_b, sqjB, ssB, rsB, stB_b, xnB, xnB_b, hTB, hTB_b)
    qsb = [sbB(f"qsb{i}", [128, 4, KC, 128], BF16) for i in range(2)]
    ksb = [sbB(f"ksb{i}", [128, 4, KC, 128], BF16) for i in range(2)]
    zsb = [sbB(f"zsb{i}", [128, 4, KC, 128], BF16) for i in range(2)]
    vsb0 = sbB("vsb0", [128, 4, D], BF16)
    vsb = [vsb0, vsb0]
    vsb0_b = Buf()
    qsb_b, ksb_b, zsb_b, vsb_b = [Buf(), Buf()], [Buf(), Buf()], [Buf(), Buf()], [vsb0_b, vsb0_b]
    stgB_t = [qsb[0], ksb[0], zsb[0], qsb[1], ksb[1], zsb[1]]
    stgB = [t_[:].rearrange("p a b c -> p (a b c)").bitcast(F32).rearrange("p (k n) -> p k n", k=KC) for t_ in stgB_t]
    load_weight_bf16(Win, Win_b, winn_d, 4 * D, stgB, [qsb_b[0], ksb_b[0], zsb_b[0], qsb_b[1], ksb_b[1], zsb_b[1]],
                     256)

    def x1buf(tok):
        return dbuf((id(x1_d), tok))

    def ctx1buf(tok):
        return dbuf((id(ctx1_d), tok))

    evr = [0]

    def evac_copy(out_ap, in_ap, reads, writes):
        eng = ("dve", "act")[evr[0] % 2]
        evr[0] += 1
        if eng == "act":
            P.op("act", lambda e: e.copy(out=out_ap, in_=in_ap), reads=reads, writes=writes)
        else:
            P.op("dve", lambda e: e.tensor_copy(out=out_ap, in_=in_ap), reads=reads, writes=writes)

    front_b(fbB, front_a(fbB, ctx1_d, 0, 2, src_bufs=ctx1buf), 0, 1, 1)
    xslotsB = {0: front_a(fbB, x1_d, 0, 4, src_bufs=x1buf)}
    front_b(fbB, xslotsB[0], 1, 1, 0)
    xslotsB[1] = front_a(fbB, x1_d, 512, 4, src_bufs=x1buf)
    for oc in range(KC):
        bank, bb = proj(hTB[0], hTB_b[0][:2], LC, D + oc * 128)
        evac_copy(kcT[:, oc, :], bank[:, 0:LC], [bb], [kcT_b])
    for tt in range(2):
        for h in range(2):
            bank, bb = next_pp()

            def mmv(e, bank=bank, tt=tt, h=h):
                ins = None
                for k in range(KC):
                    ins = e.matmul(bank[:, :], lhsT=hTB[0][:, k, tt * 128:(tt + 1) * 128],
                                   rhs=Win[:, k, 2 * D + h * 512: 2 * D + (h + 1) * 512],
                                   start=(k == 0), stop=(k == KC - 1))
                return ins

            P.op("pe", mmv, reads=[Win_b] + hTB_b[0][:2], writes=[bb])
            evac_copy(vc[:, tt, h * 512:(h + 1) * 512], bank[:, :], [bb], [vc_b])

    nblkB = NB if stop_after not in ("small", "AB_small") else 2
    scr_b = {}

    def sbuf_(key):
        if key not in scr_b:
            scr_b[key] = Buf(str(key))
        return scr_b[key]

    for j in range(nblkB):
        slot = (j + 1) % 3
        ob = j % 2
        for oc in range(KC):
            if 2 <= oc <= 5 and j + 1 < nblkB:
                front_b(fbB, xslotsB[j + 1], (j + 2) % 3, 1, 0, only=oc - 2)
            if oc == 6 and j + 2 < nblkB:
                xslotsB[j + 2] = front_a(fbB, x1_d, (j + 2) * 512, 4, src_bufs=x1buf)
            bank, bb = proj(hTB[slot], hTB_b[slot], 512, oc * 128)
            P.op("act", lambda e, bank=bank, ob=ob, oc=oc: e.activation(
                out=qsb[ob][:, :, oc, :], in_=bank[:, :].rearrange("p (t n) -> p t n", t=4), func=AF.Copy,
                scale=0.125), reads=[bb], writes=[qsb_b[ob]])
            bank, bb = proj(hTB[slot], hTB_b[slot], 512, D + oc * 128)
            P.op("dve", lambda e, bank=bank, ob=ob, oc=oc: e.tensor_copy(
                out=ksb[ob][:, :, oc, :], in_=bank[:, :].rearrange("p (t n) -> p t n", t=4)),
                reads=[bb], writes=[ksb_b[ob]])
            bank, bb = proj(hTB[slot], hTB_b[slot], 512, 3 * D + oc * 128)
            P.op("act", lambda e, bank=bank, ob=ob, oc=oc: e.activation(
                out=zsb[ob][:, :, oc, :], in_=bank[:, :].rearrange("p (t n) -> p t n", t=4), func=AF.Silu),
                reads=[bb], writes=[zsb_b[ob]])
        for tt in range(4):
            for h in range(2):
                bank, bb = next_pp()

                def mmv(e, bank=bank, tt=tt, h=h, slot=slot):
                    ins = None
                    for k in range(KC):
                        ins = e.matmul(bank[:, :], lhsT=hTB[slot][:, k, tt * 128:(tt + 1) * 128],
                                       rhs=Win[:, k, 2 * D + h * 512: 2 * D + (h + 1) * 512],
                                       start=(k == 0), stop=(k == KC - 1))
                    return ins

                P.op("pe", mmv, reads=[Win_b] + hTB_b[slot], writes=[bb])
                evac_copy(vsb[ob][:, tt, h * 512:(h + 1) * 512], bank[:, :], [bb], [vsb_b[ob]])
        P.dma("sp", f"d_qo{ob}", qT_d[j], qsb[ob][:].rearrange("p a b c -> p (a b c)"), reads=[qsb_b[ob]],
              writes=[sbuf_(("q", j))])
        P.dma("sp", f"d_ko{ob}", kT_d[j], ksb[ob][:].rearrange("p a b c -> p (a b c)"), reads=[ksb_b[ob]],
              writes=[sbuf_(("k", j))])
        P.dma("sp", f"d_zo{ob}", szT_d[j], zsb[ob][:].rearrange("p a b c -> p (a b c)"), reads=[zsb_b[ob]],
              writes=[sbuf_(("z", j))])
        P.dma("sp", f"d_vo{ob}", v_d[j * 512:(j + 1) * 512, :].rearrange("(t p) d -> p t d", p=128), vsb[ob][:],
              reads=[vsb_b[ob]], writes=[sbuf_(("v", j))])

    P.barrier()
    if stop_after == "AB_small":
        with nc.Block() as block:
            P.emit(block)
        esB.close()
        esBC.close()
        return
    esB.close()

    esC = ExitStack()

    def sbC(name, shape, dt=F32):
        return esC.enter_context(nc.sbuf_tensor("sbC_" + name, list(shape), dt))

    NH = 16
    mtab = Win[:].rearrange("p k n -> p (k n)")[:, 0:NH * 24 * 64].rearrange("p (h x) -> p h x", h=NH)
    mtab_b = Buf("mtab")
    stgC = [sbC(f"stgC{i}", [128, 2048]) for i in range(2)]
    stgC_b = [Buf(), Buf()]
    load_weight_bf16(Wout, Wout_b, woutn_d, D,
                     [stgC[0][:].rearrange("p (k n) -> p k n", k=KC), stgC[1][:].rearrange("p (k n) -> p k n", k=KC)],
                     stgC_b, 256)
    for i in range(12):
        s_ = i % 2
        P.dma(("sp", "act")[i % 2], f"d_wst{s_}", stgC[s_][:], rpbG_d[:, i * 2048:(i + 1) * 2048],
              writes=[stgC_b[s_]])
        P.op("act", lambda e, s_=s_, i=i: e.activation(
            out=Win[:].rearrange("p k n -> p (k n)")[:, i * 2048:(i + 1) * 2048], in_=stgC[s_][:], func=AF.Exp),
            reads=[stgC_b[s_]], writes=[mtab_b])

    KR = 8
    kring = [sbC(f"kring{i}", [128, KC, 128], BF16) for i in range(KR)]
    vring = [sbC(f"vring{i}", [128, D], BF16) for i in range(KR)]
    kring_b = [Buf() for i in range(KR)]
    vring_b = [Buf() for i in range(KR)]
    qtl = [sbC(f"qtl{i}", [128, KC, 128], BF16) for i in range(2)]
    ztl = [sbC(f"ztl{i}", [128, KC, 128], BF16) for i in range(2)]
    x1t = [sbC(f"x1t{i}", [128, D]) for i in range(2)]
    qtl_b, ztl_b, x1t_b = [Buf(), Buf()], [Buf(), Buf()], [Buf(), Buf()]
    pT = [sbC(f"pT{i}", [128, 2, 896], BF16) for i in range(2)]
    pT_b = [Buf(), Buf()]
    yTC = [sbC(f"yTC{i}", [128, KC, 128], BF16) for i in range(2)]
    yTC_b = [[Buf() for c in range(KC)] for i in range(2)]
    rcC = [sbC(f"rcC{i}", [128, 128]) for i in range(2)]
    wC = [sbC(f"wC{i}", [128, 128]) for i in range(2)]
    rcC_b, wC_b = [Buf(), Buf()], [Buf(), Buf()]
    tmpC = [sbC(f"tmpC{i}", [128, D]) for i in range(2)]
    tmpC_b = [Buf(), Buf()]
    sqjC = sbC("sqjC", [128, 512], BF16)
    ss2C = sbC("ss2C", [128, 8, 2])
    rs2C = sbC("rs2C", [128, 8])
    st2C_b = [Buf() for i in range(8)]
    sS = psS[:].rearrange("p a n -> p (a n)")
    sS_b = [Buf("sA"), Buf("sB")]
    pOs = [pp4[:, 0:256], ptp[:].rearrange("p a b -> p (a b)").bitcast(F32)[:, 0:256]]
    pO_b = [Buf("pO0"), Buf("pO1")]

    NT = L // 128
    ntC = NT if stop_after != "small" else 6

    def rs_of(r):
        return min(max(r - 4, 0), 120)

    def keytiles(t):
        lo = rs_of(2 * t) // 2
        hi = (rs_of(2 * t + 1) + 7) // 2
        return list(range(hi, lo - 1, -1))

    def mask_off(t):
        kts = keytiles(t)
        if len(kts) == 5:
            return 0, 5
        dmax = kts[0] - t
        ti = 6 - 2 * dmax
        return (10 + ti) * 64, 4

    loaded = set()

    def ensure_keys(t):
        for kt in sorted(keytiles(t)):
            if kt in loaded:
                continue
            loaded.add(kt)
            sl = kt % KR
            P.dma("sp", f"d_kr{sl}", kring[sl][:].rearrange("p c n -> p (c n)"),
                  kT_d[kt // 4, :, (kt % 4) * 1024:(kt % 4 + 1) * 1024],
                  reads=[sbuf_(("k", kt // 4))], writes=[kring_b[sl]])
            P.dma("sp", f"d_vr{sl}", vring[sl][:], v_d[kt * 128:(kt + 1) * 128, :],
                  reads=[sbuf_(("v", kt // 4))], writes=[vring_b[sl]])

    def load_tile(t):
        s_ = t % 2
        P.dma("sp", f"d_ql{s_}", qtl[s_][:].rearrange("p c n -> p (c n)"),
              qT_d[t // 4, :, (t % 4) * 1024:(t % 4 + 1) * 1024], reads=[sbuf_(("q", t // 4))], writes=[qtl_b[s_]])
        P.dma("sp", f"d_zl{s_}", ztl[s_][:].rearrange("p c n -> p (c n)"),
              szT_d[t // 4, :, (t % 4) * 1024:(t % 4 + 1) * 1024], reads=[sbuf_(("z", t // 4))], writes=[ztl_b[s_]])
        P.dma("sp", f"d_xl{s_}", x1t[s_][:], x1_d[t * 128:(t + 1) * 128, :], reads=[x1buf(t * 128)],
              writes=[x1t_b[s_]])

    pTh_b = [[Buf(), Buf()], [Buf(), Buf()]]

    def s_exp(t, c, hh):
        ts_ = t % 2
        kts = keytiles(t)
        nk = len(kts)
        ncol = (2 + nk) * 128
        pb = c % 2
        p0 = hh * 64

        def mm(e):
            ins = None
            for s_i in range(2 + nk):
                if s_i < 2:
                    lhs = kcT[p0:p0 + 64, c, s_i * 128:(s_i + 1) * 128]
                else:
                    lhs = kring[kts[s_i - 2] % KR][p0:p0 + 64, c, :]
                ins = e.matmul(sS[:, hh * 1024 + s_i * 128: hh * 1024 + (s_i + 1) * 128], lhsT=lhs,
                               rhs=qtl[ts_][p0:p0 + 64, c, :], start=True, stop=True)
            return ins

        P.op("pe", mm, reads=[kcT_b, qtl_b[ts_]] + [kring_b[kt % KR] for kt in kts], writes=[sS_b[hh]])
        for (a_, b_) in ((0, 512), (512, ncol)):
            P.op("act", lambda e, a_=a_, b_=b_: e.activation(
                out=pT[pb][:, hh, a_:b_], in_=sS[:, hh * 1024 + a_: hh * 1024 + b_], func=AF.Exp),
                reads=[sS_b[hh]], writes=[pTh_b[pb][hh]])

    def maskmul(t, c, hh):
        kts = keytiles(t)
        nk = len(kts)
        ncol = (2 + nk) * 128
        pb = c % 2
        off, nk2 = mask_off(t)
        P.op("dve", lambda e: e.tensor_tensor(
            out=pT[pb][:, hh, 256:ncol], in0=pT[pb][:, hh, 256:ncol],
            in1=mtab[:, 2 * c + hh, off:off + nk * 128], op=ALU.mult),
            reads=[pTh_b[pb][hh], mtab_b], writes=[pTh_b[pb][hh]])

    def pv(t, c, hh):
        kts = keytiles(t)
        nk = len(kts)
        pb = c % 2
        ob = c % 2
        O = pOs[ob][:, 0:128]
        SM = pOs[ob][:, 128:256]
        p0 = hh * 64

        def mm(e):
            ins = None
            n = 2 + nk
            for s_i in range(n):
                if s_i < 2:
                    lhs = vc[:, s_i, c * 128 + p0: c * 128 + p0 + 64]
                else:
                    lhs = vring[kts[s_i - 2] % KR][:, c * 128 + p0: c * 128 + p0 + 64]
                rhs = pT[pb][:, hh, s_i * 128:(s_i + 1) * 128]
                e.matmul(O[p0:p0 + 64, :], lhsT=lhs, rhs=rhs, start=(s_i == 0), stop=(s_i == n - 1),
                         tile_position=(0, p0))
                ins = e.matmul(SM[p0:p0 + 64, :], lhsT=ones64[:, :], rhs=rhs, start=False,
                               stop=(s_i == n - 1), tile_position=(0, p0), skip_group_check=True)
            return ins

        P.op("pe", mm, reads=[vc_b, pTh_b[pb][hh], b_const] + [vring_b[kt % KR] for kt in kts], writes=[pO_b[ob]])

    def norm_items(t, c):
        ts_ = t % 2
        ob = c % 2
        O = pOs[ob][:, 0:128]
        SM = pOs[ob][:, 128:256]

        def r0():
            P.op("dve", lambda e: e.reciprocal(out=rcC[ob][:, 0:64], in_=SM[:, 0:64]), reads=[pO_b[ob]],
                 writes=[rcC_b[ob]])

        def r1():
            P.op("dve", lambda e: e.reciprocal(out=rcC[ob][:, 64:128], in_=SM[:, 64:128]), reads=[pO_b[ob]],
                 writes=[rcC_b[ob]])
            P.op("pool", lambda e: e.tensor_tensor(out=wC[ob][:], in0=rcC[ob][:], in1=ztl[ts_][:, c, :],
                                                   op=ALU.mult),
                 reads=[rcC_b[ob], ztl_b[ts_]], writes=[wC_b[ob]])

        def y_():
            P.op("dve", lambda e: e.tensor_tensor(out=yTC[ts_][:, c, :], in0=O, in1=wC[ob][:], op=ALU.mult),
                 reads=[pO_b[ob], wC_b[ob]], writes=[yTC_b[ts_][c]])
            if c == KC - 1:
                outproj(t)

        return [r0, r1, y_]

    def outproj(t):
        ts_ = t % 2
        si = t % 8
        for h in range(2):
            def mmo(e, h=h):
                ins = None
                for k in range(KC):
                    ins = e.matmul(py[:, h, :], lhsT=yTC[ts_][:, k, :], rhs=Wout[:, k, h * 512:(h + 1) * 512],
                                   start=(k == 0), stop=(k == KC - 1))
                return ins

            P.op("pe", mmo, reads=[Wout_b] + yTC_b[ts_], writes=[py_b[h]])
            P.op("act", lambda e, h=h: e.activation(out=sqjC[:], in_=py[:, h, :], func=AF.Square,
                                                    scale=1.0 / 32.0, accum_out=ss2C[:, si, h:h + 1]),
                 reads=[py_b[h]], writes=[st2C_b[si]])
        P.op("dve", lambda e: e.tensor_tensor(out=ss2C[:, si, 0:1], in0=ss2C[:, si, 0:1], in1=ss2C[:, si, 1:2],
                                              op=ALU.add), reads=[st2C_b[si]], writes=[st2C_b[si]])
        rstd_chain(ss2C[:, si, 0:1], rs2C[:, si:si + 1], st2C_b[si])
        def stt(h):
            P.op("dve", lambda e: e.scalar_tensor_tensor(
                out=tmpC[ts_][:, h * 512:(h + 1) * 512], in0=py[:, h, :], scalar=rs2C[:, si:si + 1],
                in1=GT[:, 2, h * 512:(h + 1) * 512], op0=ALU.mult, op1=ALU.mult),
                reads=[py_b[h], st2C_b[si], GT_b], writes=[tmpC_b[ts_]])

        def fin():
            for h in range(2):
                P.op("dve", lambda e, h=h: e.tensor_tensor(
                    out=tmpC[ts_][:, h * 512:(h + 1) * 512], in0=tmpC[ts_][:, h * 512:(h + 1) * 512],
                    in1=x1t[ts_][:, h * 512:(h + 1) * 512], op=ALU.add),
                    reads=[tmpC_b[ts_], x1t_b[ts_]], writes=[tmpC_b[ts_]])
            P.dma("sp", f"d_oo{ts_}", out_d[t * 128:(t + 1) * 128, :], tmpC[ts_][:], reads=[tmpC_b[ts_]],
                  writes=[dbuf(("out", t))])

        late.append((cur_step[0] + 4, lambda: stt(0)))
        late.append((cur_step[0] + 4, lambda: stt(1)))
        late.append((cur_step[0] + 5, fin))

    late = []
    cur_step = [0]
    items = [(t, c, hh) for t in range(ntC) for c in range(KC) for hh in range(2)]
    deferred = []
    ensure_keys(0)
    load_tile(0)
    LOOK = 2
    for n in range(min(LOOK, len(items))):
        s_exp(*items[n])
    for n, (t, c, hh) in enumerate(items):
        cur_step[0] = n
        if c == 4 and hh == 0 and t + 1 < ntC:
            ensure_keys(t + 1)
            load_tile(t + 1)
        if n + LOOK < len(items):
            s_exp(*items[n + LOOK])
        maskmul(t, c, hh)
        pv(t, c, hh)
        for _ in range(2):
            if deferred:
                deferred.pop(0)()
        if late and late[0][0] <= n:
            late.pop(0)[1]()
        if hh == 1:
            deferred.extend(norm_items(t, c))
    while deferred:
        deferred.pop(0)()
    while late:
        late.pop(0)[1]()
    P.wait_all("sp", [dbuf(("out", t)) for t in range(ntC)])
    P.barrier()
    with nc.Block() as block:
        P.emit(block)
    esC.close()
    esBC.close()


def _rpb_gather(rpb):
    NEG = np.float32(-30000.0)
    out = np.full((128, 16, 24, 64), NEG, dtype=np.float32)
    qc = np.arange(64)
    cs = np.clip(qc - 8, 0, 48)
    for kl in range(2):
        for kc in range(64):
            p = kl * 64 + kc
            valid_q = (cs <= kc) & (kc < cs + 16)
            dc = kc - qc + 15
            qv = qc[valid_q]
            for b in range(24):
                if b < 10:
                    dr = 4 - b + kl
                    if dr < -4 or dr > 3:
                        continue
                else:
                    dr = 6 - (b - 10) + kl
                    if dr < -7 or dr > 7:
                        continue
                out[p, :, b, qv] = rpb[:, dr + 7, dc[valid_q]].T
    return np.ascontiguousarray(out.reshape(128, 16 * 24 * 64))


def prep_inputs(inputs):
    f = lambda a: np.ascontiguousarray(np.asarray(a, dtype=np.float32))
    x, c, ctx, c_ctx = f(inputs["x"]), f(inputs["c"]), f(inputs["ctx"]), f(inputs["c_ctx"])
    g_pre, g_post, w_mod, b_mod = f(inputs["g_pre"]), f(inputs["g_post"]), f(inputs["w_mod"]), f(inputs["b_mod"])
    conv_w = f(inputs["conv_w"])
    shared = {
        "w_mod": w_mod,
        "bmodT": f(b_mod.reshape(2, 24, 128).transpose(2, 0, 1).reshape(128, 48)),
        "bgate": f(b_mod[:, 2 * D:3 * D]),
        "gpreT": f(g_pre.reshape(2, 8, 128).transpose(2, 0, 1).reshape(128, 16)),
        "gpost": g_post,
        "w_in_conv": f(inputs["w_in_conv"][0]),
        "w_out_conv": f(inputs["w_out_conv"][0]),
        "w_in_na": f(inputs["w_in_na"][0]),
        "w_out_na": f(inputs["w_out_na"][0]),
        "convwT": f(conv_w[0].reshape(3, 8, 128).transpose(2, 1, 0).reshape(128, 24)),
        "identd": np.eye(128, dtype=np.float32),
        "rpbG": _rpb_gather(f(inputs["rpb"][0])),
    }
    in_maps = []
    for b in range(NCORES):
        cv2 = np.stack([c[b], c_ctx], 0)
        m = dict(shared)
        m["x"] = x[b]
        m["ctx"] = ctx[b]
        m["cT"] = f(cv2.reshape(2, 8, 128).transpose(2, 0, 1).reshape(128, 16))
        in_maps.append(m)
    return in_maps


def kernel(**inputs):
    nc = build("all")
    in_maps = prep_inputs(inputs)
    res = run_bass_kernel_spmd(nc, in_maps, core_ids=list(range(NCORES)))
    return np.stack([r["out"] for r in res.results], 0)
```

```python
import os
import numpy as np
from contextlib import ExitStack
import concourse.bass as bass
import concourse.mybir as mybir
from concourse.bass_utils import run_bass_kernel_spmd

F32 = mybir.dt.float32
BF16 = mybir.dt.bfloat16
AF = mybir.ActivationFunctionType
ALU = mybir.AluOpType

D = 1024
L = 8192
LC = 256
NCORES = 8
EPS = 1e-6
KC = 8
NB = L // 512


class Buf:
    __slots__ = ("w", "r", "name")

    def __init__(self, name=""):
        self.w = None
        self.r = {}
        self.name = name


class Prog:
    ENG = ("pe", "act", "dve", "pool", "sp")

    def __init__(self, nc, es):
        self.nc = nc
        self.es = es
        self.q = {e: [] for e in self.ENG}
        self.cnt = {}
        self.sems = {}
        self.seen = {e: {} for e in self.ENG}
        for e in ("pe", "act", "dve", "pool"):
            self._sem("c_" + e)

    def _sem(self, key):
        if key not in self.sems:
            self.sems[key] = self.es.enter_context(self.nc.semaphore(key))
            self.cnt[key] = 0
        return self.sems[key]

    def _deps(self, eng, reads, writes):
        deps = []
        for b in reads:
            if b.w is not None:
                deps.append(b.w)
        for b in writes:
            if b.w is not None and b.w[2] != eng:
                deps.append(b.w)
            for t in b.r.values():
                if t[2] != eng:
                    deps.append(t)
        for (k, v, src) in deps:
            if src == "pe" and eng == "pe":
                continue
            if self.seen[eng].get(k, 0) >= v:
                continue
            self.seen[eng][k] = v
            self.q[eng].append(("wait", k, v))

    def _commit(self, tok, reads, writes):
        for b in reads:
            old = b.r.get(tok[0])
            if old is None or old[1] < tok[1]:
                b.r[tok[0]] = tok
        for b in writes:
            b.w = tok
            b.r = {}

    def op(self, eng, fn, reads=(), writes=()):
        self._deps(eng, reads, writes)
        key = "c_" + eng
        self.cnt[key] += 1
        tok = (key, self.cnt[key], eng)
        self.q[eng].append(("op", fn, key, 1))
        self._commit(tok, reads, writes)
        return tok

    def dma(self, qeng, semkey, out, in_, reads=(), writes=()):
        self._sem(semkey)
        self._deps(qeng, reads, writes)
        self.cnt[semkey] += 16
        tok = (semkey, self.cnt[semkey], "dma")
        self.q[qeng].append(("op", lambda e: e.dma_start(out=out, in_=in_), semkey, 16))
        self._commit(tok, reads, writes)
        return tok

    def barrier(self):
        for e in self.ENG:
            for k, v in self.cnt.items():
                if v > 0 and self.seen[e].get(k, 0) < v:
                    self.seen[e][k] = v
                    self.q[e].append(("wait", k, v))

    def wait_all(self, eng, bufs):
        self._deps(eng, bufs, ())

    def emit(self, block):
        def run(e, items):
            for it in items:
                if it[0] == "wait":
                    e.wait_ge(self.sems[it[1]], it[2])
                else:
                    ins = it[1](e)
                    ins.then_inc(self.sems[it[2]], it[3])

        block.tensor(lambda e: run(e, self.q["pe"]))
        block.scalar(lambda e: run(e, self.q["act"]))
        block.vector(lambda e: run(e, self.q["dve"]))
        block.gpsimd(lambda e: run(e, self.q["pool"]))
        block.sync(lambda e: run(e, self.q["sp"]))


def build(stop_after="all"):
    nc = bass.Bass("TRN2", target_bir_lowering=False)
    es = ExitStack()
    with es:
        _build(nc, es, stop_after)
    return nc


def _build(nc, es, stop_after):
    P = Prog(nc, es)

    def din(name, shape, dt=F32):
        return nc.dram_tensor(name, list(shape), dt, kind="ExternalInput").ap()

    x_d = din("x", [L, D])
    ctx_d = din("ctx", [LC, D])
    cT_d = din("cT", [128, 16])
    wmod_d = din("w_mod", [2, D, 3 * D])
    bmodT_d = din("bmodT", [128, 48])
    bgate_d = din("bgate", [2, D])
    gpreT_d = din("gpreT", [128, 16])
    gpost_d = din("gpost", [2, D])
    winc_d = din("w_in_conv", [D, 4 * D])
    woutc_d = din("w_out_conv", [D, D])
    winn_d = din("w_in_na", [D, 4 * D])
    woutn_d = din("w_out_na", [D, D])
    convwT_d = din("convwT", [128, 24])
    out_d = nc.dram_tensor("out", [L, D], F32, kind="ExternalOutput").ap()
    x1_d = nc.dram_tensor("x1_scr", [L, D], F32).ap()
    ctx1_d = nc.dram_tensor("ctx1_scr", [LC, D], F32).ap()
    qT_d = nc.dram_tensor("qT_scr", [NB, 128, 4 * KC * 128], BF16).ap()
    kT_d = nc.dram_tensor("kT_scr", [NB, 128, 4 * KC * 128], BF16).ap()
    szT_d = nc.dram_tensor("szT_scr", [NB, 128, 4 * KC * 128], BF16).ap()
    v_d = nc.dram_tensor("v_scr", [L, D], BF16).ap()
    rpbG_d = din("rpbG", [128, 16 * 24 * 64])

    def sb(name, shape, dt=F32):
        return es.enter_context(nc.sbuf_tensor("sb_" + name, list(shape), dt))

    def ps(name, shape, dt=F32):
        return es.enter_context(nc.psum_tensor("ps_" + name, list(shape), dt))

    ident = sb("ident", [128, 128], BF16)
    ones1 = sb("ones1", [1, 128], F32)
    cT = sb("cT", [128, 16])
    siluT = sb("siluT", [128, 16])
    bmodT = sb("bmodT_sb", [128, 48])
    gpreT = sb("gpreT_sb", [128, 16])
    convw = sb("convw_sb", [128, 24])
    Win = sb("Win", [128, KC, 4 * D], BF16)
    Wout = sb("Wout", [128, KC, D], BF16)
    Win_b, Wout_b = Buf("Win"), Buf("Wout")
    ones64 = sb("ones64", [128, 64], BF16)
    cst = sb("cst", [128, 2])
    kcT_b, vc_b = Buf("kcT"), Buf("vc")
    modT = sb("modT", [128, 2, 16, 2])
    Gs = sb("Gs", [128, 2, 2, 8])
    SHs = sb("SHs", [128, 2, 2, 8])
    GT = sb("GT", [128, 3, D])
    es_mod = ExitStack()

    def sbm(name, shape, dt=F32):
        return es_mod.enter_context(nc.sbuf_tensor("sm_" + name, list(shape), dt))

    identf = sbm("identf", [128, 128], F32)
    silu_bc = sbm("silu_bc", [128, 16, 128])
    bgate = sbm("bgate_sb", [1, 2 * D])
    gpost_bc = sbm("gpost_bc", [128, 2, D])

    b_const = Buf("const")

    psS = ps("psS", [128, 4, 512])
    pp4 = ps("pp4", [128, 512])
    pp = [psS[:, i, :] for i in range(4)] + [pp4[:, :]]
    pp_b = [Buf(f"pp{i}") for i in range(5)]
    py = ps("py", [128, 2, 512])
    py_b = [Buf("py0"), Buf("py1")]
    ptp = ps("ptp", [128, 8, 128], BF16)
    ptp_b = Buf("ptp")
    pp_rr = [0]

    def next_pp():
        i = pp_rr[0] % 4
        pp_rr[0] += 1
        return pp[i], pp_b[i]

    ptps = [ptp[:], pp4[:].bitcast(BF16).rearrange("p (k n) -> p k n", k=KC)]
    ptps_b = [ptp_b, pp_b[4]]
    tp_rr = [0]

    def mk_ident(e):
        e.memset(ones64[:], 1.0)
        e.memset(cst[:, 0:1], float(EPS))
        e.memset(cst[:, 1:2], -0.5)
        return e.memset(ones1[:], 1.0)

    P.op("pool", mk_ident, writes=[b_const])
    ld_b = Buf("ld_small")
    P.dma("sp", "d_small", cT[:], cT_d, writes=[ld_b])
    P.dma("sp", "d_small", bmodT[:], bmodT_d, writes=[ld_b])
    P.dma("sp", "d_small", gpreT[:], gpreT_d, writes=[ld_b])
    P.dma("sp", "d_small", convw[:], convwT_d, writes=[ld_b])
    P.dma("sp", "d_small", bgate[:], bgate_d.rearrange("(o a) d -> o (a d)", o=1), writes=[ld_b])
    P.dma("sp", "d_small", gpost_bc[:].rearrange("p a d -> p (a d)"),
          gpost_d.rearrange("(o a) d -> o (a d)", o=1).partition_broadcast(128), writes=[ld_b])
    identd = din("identd", [128, 128])
    P.dma("sp", "d_small", identf[:], identd, reads=[b_const], writes=[ld_b])

    siluT_b = Buf("siluT")
    P.op("act", lambda e: e.activation(out=siluT[:], in_=cT[:], func=AF.Silu), reads=[ld_b], writes=[siluT_b])
    ident_b = Buf("ident")
    P.op("dve", lambda e: e.tensor_copy(out=ident[:], in_=identf[:]), reads=[ld_b], writes=[ident_b])
    silu_bc_b = Buf("silu_bc")
    P.op("dve", lambda e: e.tensor_copy(out=silu_bc[:], in_=siluT[:].unsqueeze(2).to_broadcast([128, 16, 128])),
         reads=[siluT_b], writes=[silu_bc_b])

    NWST = 4
    wst = [sbm(f"wst{i}", [128, KC, 512], F32) for i in range(NWST)]
    wst_b = [Buf(f"wst{i}") for i in range(NWST)]
    modT_b = Buf("modT")
    GT_b = Buf("GT")
    cvt_rr = [0]
    ci_holder = [0]
    wtag = [0]

    def weight_chunks(dst, dst_b, src_d, ncols, stg, stg_b, wcol):
        out = []
        wtag[0] += 1
        tag = wtag[0]
        for c in range(ncols // wcol):
            def emit(c=c):
                s = ci_holder[0] % len(stg)
                ci_holder[0] += 1
                P.dma(("sp", "act")[ci_holder[0] % 2], f"d_wl{tag % 2}_{s}", stg[s],
                      src_d[:, c * wcol:(c + 1) * wcol].rearrange("(k p) n -> p k n", p=128), writes=[stg_b[s]])
                eng = ("dve", "act")[cvt_rr[0] % 2]
                cvt_rr[0] += 1
                if eng == "act":
                    P.op("act", lambda e: e.copy(out=dst[:, :, c * wcol:(c + 1) * wcol], in_=stg[s]),
                         reads=[stg_b[s]], writes=[dst_b])
                else:
                    P.op("dve", lambda e: e.tensor_copy(out=dst[:, :, c * wcol:(c + 1) * wcol], in_=stg[s]),
                         reads=[stg_b[s]], writes=[dst_b])
            out.append(emit)
        return out

    def load_weight_bf16(dst, dst_b, src_d, ncols, stg, stg_b, wcol):
        for f_ in weight_chunks(dst, dst_b, src_d, ncols, stg, stg_b, wcol):
            f_()

    stgA = [wst[2][:], wst[3][:]]
    stgA_b = [wst_b[2], wst_b[3]]
    pendingA = (weight_chunks(Win, Win_b, winc_d, 4 * D, stgA, stgA_b, 512)
                + weight_chunks(Wout, Wout_b, woutc_d, D, stgA, stgA_b, 512))
    ci = 0
    for layer in range(2):
        for nch in range(6):
            for _ in range(4):
                if pendingA:
                    pendingA.pop(0)()
            s = ci % 2
            ci += 1
            P.dma(("sp", "act")[ci % 2], f"d_wst{s}", wst[s][:],
                  wmod_d[layer, :, nch * 512:(nch + 1) * 512].rearrange("(k p) n -> p k n", p=128),
                  writes=[wst_b[s]])
            if nch < 4:
                bank, bb = next_pp()

                def mm(e, s=s, bank=bank):
                    ins = None
                    for m in range(4):
                        for k in range(KC):
                            ins = e.matmul(bank[:, m * 2:m * 2 + 2], lhsT=wst[s][:, k, m * 128:(m + 1) * 128],
                                           rhs=siluT[:].rearrange("p (w k) -> p k w", w=2)[:, k, :],
                                           start=(k == 0), stop=(k == KC - 1))
                    return ins

                P.op("pe", mm, reads=[wst_b[s], siluT_b], writes=[bb])
                P.op("dve", lambda e, bank=bank, layer=layer, nch=nch: e.tensor_copy(
                    out=modT[:, layer, nch * 4:(nch + 1) * 4, :],
                    in_=bank[:, 0:8].rearrange("p (m w) -> p m w", w=2)),
                    reads=[bb], writes=[modT_b])
            else:
                half = nch - 4
                for w in range(2):
                    if layer == 1 and w == 1:
                        continue
                    gi = 0 if (layer == 0 and w == 0) else (1 if layer == 0 else 2)
                    bank, bb = next_pp()

                    def mmg(e, s=s, bank=bank, w=w, layer=layer, half=half):
                        for k in range(KC):
                            e.matmul(bank[:, :], lhsT=silu_bc[:, w * 8 + k, :], rhs=wst[s][:, k, :],
                                     start=(k == 0), stop=False)
                        return e.matmul(bank[:, :], lhsT=ones1[:, :],
                                        rhs=bgate[:, layer * D + half * 512: layer * D + (half + 1) * 512],
                                        start=False, stop=True)

                    P.op("pe", mmg, reads=[wst_b[s], silu_bc_b, ld_b, b_const], writes=[bb])
                    P.op("dve", lambda e, bank=bank, gi=gi, layer=layer, half=half: e.scalar_tensor_tensor(
                        out=GT[:, gi, half * 512:(half + 1) * 512], in0=bank[:, :], scalar=1.0,
                        in1=gpost_bc[:, layer, half * 512:(half + 1) * 512], op0=ALU.mult, op1=ALU.mult),
                        reads=[bb, ld_b], writes=[GT_b])
    GS_b = Buf("GS")
    for layer in range(2):
        for w in range(2):
            P.op("dve", lambda e, layer=layer, w=w: e.tensor_tensor(
                out=SHs[:, layer, w, :], in0=modT[:, layer, 0:8, w], in1=bmodT[:, layer * 24:layer * 24 + 8],
                op=ALU.add), reads=[modT_b, ld_b], writes=[GS_b])
            P.op("dve", lambda e, layer=layer, w=w: e.scalar_tensor_tensor(
                out=Gs[:, layer, w, :], in0=modT[:, layer, 8:16, w], scalar=1.0,
                in1=bmodT[:, layer * 24 + 8:layer * 24 + 16], op0=ALU.add, op1=ALU.add),
                reads=[modT_b, ld_b], writes=[GS_b])
            P.op("dve", lambda e, layer=layer, w=w: e.scalar_tensor_tensor(
                out=Gs[:, layer, w, :], in0=Gs[:, layer, w, :], scalar=1.0,
                in1=gpreT[:, layer * 8:(layer + 1) * 8], op0=ALU.mult, op1=ALU.mult),
                reads=[GS_b, ld_b], writes=[GS_b])

    ci_holder = [ci]

    while pendingA:
        pendingA.pop(0)()
    P.barrier()
    es_mod.close()

    esA = ExitStack()

    def sbA(name, shape, dt=F32):
        return esA.enter_context(nc.sbuf_tensor("sa_" + name, list(shape), dt))

    NX = 2
    xin = [sbA(f"xin{i}", [128, D]) for i in range(NX)]
    xin_b = [Buf(f"xin{i}") for i in range(NX)]
    sqj = sbA("sqj", [128, D], BF16)
    ss = sbA("ss", [128, 8])
    rs = sbA("rs", [128, 8])
    st_b = [Buf(f"st{i}") for i in range(8)]
    xn = [sbA(f"xn{i}", [128, D], BF16) for i in range(4)]
    xn_b = [Buf(f"xn{i}") for i in range(4)]
    hT = [sbA(f"hT{i}", [128, KC, 512], BF16) for i in range(3)]
    hT_b = [[Buf(f"hT{i}_{t}") for t in range(4)] for i in range(3)]
    cu = [sbA(f"cu{i}", [128, KC, 514]) for i in range(2)]
    cu_b = [[Buf(f"cu{i}_{c}") for c in range(KC)] for i in range(2)]
    hl_b = [Buf("hl0"), Buf("hl1")]
    hr_b = [Buf("hr0"), Buf("hr1")]
    cg = [sbA(f"cg{i}", [128, 512]) for i in range(2)]
    cg_b = [Buf("cg0"), Buf("cg1")]
    szb = [sbA(f"sz{i}", [128, 512]) for i in range(2)]
    sz_b = [Buf("sz0"), Buf("sz1")]
    cv = [sbA(f"cv{i}", [128, 512]) for i in range(2)]
    cv_b = [Buf("cv0"), Buf("cv1")]
    yT = sbA("yT", [128, KC, 512], BF16)
    yT_b = [Buf(f"yT{c}") for c in range(KC)]
    tmp = [sbA(f"tmp{i}", [128, D]) for i in range(2)]
    tmp_b = [Buf("tmp0"), Buf("tmp1")]
    xres = [sbA(f"xres{i}", [128, D]) for i in range(NX)]
    xres_b = [Buf(f"xres{i}") for i in range(NX)]
    ss2 = sbA("ss2", [128, 8, 2])
    rs2 = sbA("rs2", [128, 8])
    st2_b = [Buf(f"st2{i}") for i in range(8)]

    cnt = {"tile": 0, "chunk": 0, "otile": 0}
    fbA = (xin, xin_b, sqj, ss, rs, st_b, xn, xn_b, hT, hT_b)

    def rstd_chain(ms_ap, rs_ap, buf):
        P.op("pool", lambda e: e.tensor_tensor(out=ms_ap, in0=ms_ap, in1=cst[:, 0:1], op=ALU.add),
             reads=[buf, b_const], writes=[buf])
        P.op("pool", lambda e: e.tensor_tensor(out=rs_ap, in0=ms_ap, in1=cst[:, 1:2], op=ALU.pow),
             reads=[buf, b_const], writes=[buf])

    NXN = 4

    def fa_stat(fb, src_d, tok, src_bufs=None):
        xin, xin_b, sqj, ss, rs, st_b, xn, xn_b, hT, hT_b = fb
        i = cnt["tile"]
        cnt["tile"] += 1
        xs = i % NX
        si = i % 8
        xb = i % NXN
        P.dma("sp", f"d_xin{xs}", xin[xs][:], src_d[tok: tok + 128, :],
              reads=([] if src_bufs is None else [src_bufs(tok)]), writes=[xin_b[xs]])
        P.op("act", lambda e: e.activation(out=sqj[:], in_=xin[xs][:], func=AF.Square,
                                           scale=1.0 / 32.0, accum_out=ss[:, si:si + 1]),
             reads=[xin_b[xs]], writes=[st_b[si]])
        rstd_chain(ss[:, si:si + 1], rs[:, si:si + 1], st_b[si])
        return (xs, si, xb)

    def fa_norm(fb, desc):
        xin, xin_b, sqj, ss, rs, st_b, xn, xn_b, hT, hT_b = fb
        xs, si, xb = desc
        P.op("act", lambda e: e.activation(out=xn[xb][:], in_=xin[xs][:], func=AF.Copy, scale=rs[:, si:si + 1]),
             reads=[xin_b[xs], st_b[si]], writes=[xn_b[xb]])
        return xb

    def front_a(fb, src_d, tok0, ntile, src_bufs=None):
        slots = []
        prev = None
        for tt in range(ntile):
            dsc = fa_stat(fb, src_d, tok0 + tt * 128, src_bufs)
            if prev is not None:
                slots.append(fa_norm(fb, prev))
            prev = dsc
        slots.append(fa_norm(fb, prev))
        return slots

    def front_b(fb, slots, slot, layer, w, only=None):
        xin, xin_b, sqj, ss, rs, st_b, xn, xn_b, hT, hT_b = fb
        for tt, xb in enumerate(slots):
            if only is not None and tt != only:
                continue
            pi = tp_rr[0] % 2
            tp_rr[0] += 1
            tpb = ptps[pi]
            tpb_b = ptps_b[pi]

            def tps(e, xb=xb, tpb=tpb):
                ins = None
                for k in range(KC):
                    ins = e.transpose(tpb[:, k, :], xn[xb][:, k * 128:(k + 1) * 128], ident[:])
                return ins

            P.op("pe", tps, reads=[xn_b[xb], ident_b], writes=[tpb_b])
            for k in range(KC):
                P.op("dve", lambda e, k=k, tt=tt, tpb=tpb: e.tensor_scalar(
                    out=hT[slot][:, k, tt * 128:(tt + 1) * 128], in0=tpb[:, k, :],
                    scalar1=Gs[:, layer, w, k:k + 1], scalar2=SHs[:, layer, w, k:k + 1],
                    op0=ALU.mult, op1=ALU.add),
                    reads=[tpb_b, GS_b], writes=[hT_b[slot][tt]])

    def proj(hT_t, hT_bl, nt, col0):
        bank, bb = next_pp()

        def mm(e):
            ins = None
            for k in range(KC):
                ins = e.matmul(bank[:, 0:nt], lhsT=Win[:, k, col0:col0 + 128], rhs=hT_t[:, k, 0:nt],
                               start=(k == 0), stop=(k == KC - 1))
            return ins

        P.op("pe", mm, reads=[Win_b] + list(hT_bl), writes=[bb])
        return bank, bb

    blocks = [(ctx_d, ctx1_d, 0, 2, 1, 1, True, True)]
    for j in range(NB):
        blocks.append((x_d, x1_d, j * 512, 4, 0, 0, j == 0, j == NB - 1))
    if stop_after in ("A_small", "small", "AB_small"):
        blocks = blocks[:3]
        blocks[-1] = blocks[-1][:7] + (True,)

    x1_b = {}
    def dbuf(key):
        if key not in x1_b:
            x1_b[key] = Buf(str(key))
        return x1_b[key]

    def mm1(bi, hooks=None):
        src, dst, tok0, ntile, w, gi, first, last = blocks[bi]
        slot = bi % 2
        hs = bi % 3
        nt = ntile * 128
        for oc in range(KC):
            i = cnt["chunk"]
            cnt["chunk"] += 1
            cs = i % 2
            if hooks and oc in hooks:
                for f_ in hooks[oc]:
                    f_()
            bank_c, bb_c = proj(hT[hs], hT_b[hs][:ntile], nt, D + oc * 128)
            bank_u, bb_u = proj(hT[hs], hT_b[hs][:ntile], nt, 2 * D + oc * 128)
            P.op("act", lambda e, bank_c=bank_c, cs=cs: e.copy(out=cg[cs][:, 0:nt], in_=bank_c[:, 0:nt]),
                 reads=[bb_c], writes=[cg_b[cs]])
            P.op("dve", lambda e, bank_u=bank_u, cs=cs, oc=oc: e.tensor_tensor(
                out=cu[slot][:, oc, 1:nt + 1], in0=bank_u[:, 0:nt], in1=cg[cs][:, 0:nt], op=ALU.mult),
                reads=[bb_u, cg_b[cs]], writes=[cu_b[slot][oc]])
        if first:
            P.op("pool", lambda e: e.memset(cu[slot][:, :, 0:1], 0.0), writes=[hl_b[slot]])
        else:
            ps_ = (bi - 1) % 2
            pnt = blocks[bi - 1][3] * 128
            P.op("pool", lambda e: e.tensor_copy(out=cu[slot][:, :, 0:1], in_=cu[ps_][:, :, pnt:pnt + 1]),
                 reads=cu_b[ps_], writes=[hl_b[slot]])
            P.op("pool", lambda e: e.tensor_copy(out=cu[ps_][:, :, pnt + 1:pnt + 2], in_=cu[slot][:, :, 1:2]),
                 reads=cu_b[slot], writes=[hr_b[ps_]])
        if last:
            P.op("pool", lambda e: e.memset(cu[slot][:, :, nt + 1:nt + 2], 0.0), writes=[hr_b[slot]])

    def back(bi):
        src, dst, tok0, ntile, w, gi, first, last = blocks[bi]
        slot = bi % 2
        hs = bi % 3
        nt = ntile * 128
        for oc in range(KC):
            i = cnt["chunk"]
            cnt["chunk"] += 1
            cs = i % 2
            P.op("act", lambda e, cs=cs, oc=oc: e.activation(
                out=cv[cs][:, 0:nt], in_=cu[slot][:, oc, 1:nt + 1], func=AF.Copy,
                scale=convw[:, oc * 3 + 1:oc * 3 + 2]),
                reads=[cu_b[slot][oc], ld_b], writes=[cv_b[cs]])
            P.op("dve", lambda e, cs=cs, oc=oc: e.scalar_tensor_tensor(
                out=cv[cs][:, 0:nt], in0=cu[slot][:, oc, 0:nt], scalar=convw[:, oc * 3:oc * 3 + 1],
                in1=cv[cs][:, 0:nt], op0=ALU.mult, op1=ALU.add),
                reads=[cu_b[slot][oc], hl_b[slot], cv_b[cs]], writes=[cv_b[cs]])
            P.op("dve", lambda e, cs=cs, oc=oc: e.scalar_tensor_tensor(
                out=cv[cs][:, 0:nt], in0=cu[slot][:, oc, 2:nt + 2], scalar=convw[:, oc * 3 + 2:oc * 3 + 3],
                in1=cv[cs][:, 0:nt], op0=ALU.mult, op1=ALU.add),
                reads=[cu_b[slot][oc], hr_b[slot], cv_b[cs]], writes=[cv_b[cs]])
            bank_b, bb_b = proj(hT[hs], hT_b[hs][:ntile], nt, oc * 128)
            bank_z, bb_z = proj(hT[hs], hT_b[hs][:ntile], nt, 3 * D + oc * 128)
            P.op("act", lambda e, bank_z=bank_z, cs=cs: e.activation(out=szb[cs][:, 0:nt], in_=bank_z[:, 0:nt],
                                                                      func=AF.Silu),
                 reads=[bb_z], writes=[sz_b[cs]])
            P.op("dve", lambda e, bank_b=bank_b, cs=cs: e.tensor_tensor(
                out=cv[cs][:, 0:nt], in0=bank_b[:, 0:nt], in1=cv[cs][:, 0:nt], op=ALU.mult),
                reads=[bb_b, cv_b[cs]], writes=[cv_b[cs]])
            P.op("dve", lambda e, cs=cs, oc=oc: e.tensor_tensor(
                out=yT[:, oc, 0:nt], in0=cv[cs][:, 0:nt], in1=szb[cs][:, 0:nt], op=ALU.mult),
                reads=[cv_b[cs], sz_b[cs]], writes=[yT_b[oc]])

    def outproj_tile(bi, tt):
        src, dst, tok0, ntile, w, gi, first, last = blocks[bi]
        if tt >= ntile:
            return
        if True:
            i = cnt["otile"]
            cnt["otile"] += 1
            xs = i % NX
            si = i % 8
            ts_ = i % 2
            P.dma("sp", f"d_xres{xs}", xres[xs][:], src[tok0 + tt * 128: tok0 + (tt + 1) * 128, :],
                  writes=[xres_b[xs]])
            for h in range(2):
                def mmo(e, h=h, tt=tt):
                    ins = None
                    for k in range(KC):
                        ins = e.matmul(py[:, h, :], lhsT=yT[:, k, tt * 128:(tt + 1) * 128],
                                       rhs=Wout[:, k, h * 512:(h + 1) * 512], start=(k == 0), stop=(k == KC - 1))
                    return ins

                P.op("pe", mmo, reads=[Wout_b] + yT_b, writes=[py_b[h]])
                P.op("act", lambda e, h=h, si=si: e.activation(out=sqj[:, 0:512], in_=py[:, h, :], func=AF.Square,
                                                               scale=1.0 / 32.0, accum_out=ss2[:, si, h:h + 1]),
                     reads=[py_b[h]], writes=[st2_b[si]])
            P.op("dve", lambda e, si=si: e.tensor_tensor(out=ss2[:, si, 0:1], in0=ss2[:, si, 0:1],
                                                         in1=ss2[:, si, 1:2], op=ALU.add),
                 reads=[st2_b[si]], writes=[st2_b[si]])
            rstd_chain(ss2[:, si, 0:1], rs2[:, si:si + 1], st2_b[si])
            for h in range(2):
                P.op("dve", lambda e, h=h, si=si, ts_=ts_: e.scalar_tensor_tensor(
                    out=tmp[ts_][:, h * 512:(h + 1) * 512], in0=py[:, h, :], scalar=rs2[:, si:si + 1],
                    in1=GT[:, gi, h * 512:(h + 1) * 512], op0=ALU.mult, op1=ALU.mult),
                    reads=[py_b[h], st2_b[si], GT_b], writes=[tmp_b[ts_]])
            for h in range(2):
                P.op("dve", lambda e, ts_=ts_, xs=xs, h=h: e.tensor_tensor(
                    out=tmp[ts_][:, h * 512:(h + 1) * 512], in0=tmp[ts_][:, h * 512:(h + 1) * 512],
                    in1=xres[xs][:, h * 512:(h + 1) * 512], op=ALU.add),
                    reads=[tmp_b[ts_], xres_b[xs]], writes=[tmp_b[ts_]])
            P.dma("sp", f"d_st{ts_}", dst[tok0 + tt * 128: tok0 + (tt + 1) * 128, :], tmp[ts_][:],
                  reads=[tmp_b[ts_]], writes=[dbuf((id(dst), tok0 + tt * 128))])

    nbl = len(blocks)
    xslots = {}

    def fa(b):
        if b < nbl:
            xslots[b] = front_a(fbA, blocks[b][0], blocks[b][2], blocks[b][3])

    def fbt(b, only=None):
        if b < nbl:
            front_b(fbA, xslots[b], b % 3, 0, blocks[b][4], only=only)

    fa(0)
    fbt(0)
    fa(1)
    mm1(0)
    fbt(1)
    fa(2)
    for i in range(nbl + 1):
        if i >= 1:
            back(i - 1)
        if i + 1 < nbl:
            hooks = {}
            if i >= 1:
                for k_, oc_ in enumerate((1, 3, 5, 7)):
                    hooks.setdefault(oc_, []).append(lambda k_=k_, i=i: outproj_tile(i - 1, k_))
            for tt_, oc_ in enumerate((2, 3, 4, 5)):
                hooks.setdefault(oc_, []).append(lambda i=i, tt_=tt_: fbt(i + 2, only=tt_))
            if i + 3 < nbl:
                b3 = i + 3
                st = {}

                def stat_(k, b3=b3, st=st):
                    if k < blocks[b3][3]:
                        st[k] = fa_stat(fbA, blocks[b3][0], blocks[b3][2] + k * 128)

                def norm_(k, b3=b3, st=st):
                    if k < blocks[b3][3]:
                        xslots.setdefault(b3, []).append(fa_norm(fbA, st[k]))

                hooks.setdefault(2, []).append(lambda f=stat_: f(0))
                hooks.setdefault(3, []).append(lambda f=stat_, g=norm_: (f(1), g(0)))
                hooks.setdefault(4, []).append(lambda f=stat_, g=norm_: (f(2), g(1)))
                hooks.setdefault(5, []).append(lambda f=stat_, g=norm_: (f(3), g(2)))
                hooks.setdefault(6, []).append(lambda g=norm_: g(3))
            mm1(i + 1, hooks)
        elif i >= 1:
            for k_ in range(4):
                outproj_tile(i - 1, k_)

    if stop_after in ("A", "A_small"):
        nblk = len(blocks) - 1
        for t in range(nblk * 4):
            s = t % 2
            P.dma("sp", f"d_dbg{s}", tmp[s][:], x1_d[t * 128:(t + 1) * 128, :],
                  reads=[dbuf((id(x1_d), t * 128))], writes=[tmp_b[s]])
            P.dma("sp", f"d_dbo{s}", out_d[t * 128:(t + 1) * 128, :], tmp[s][:], reads=[tmp_b[s]],
                  writes=[dbuf(("out", t))])
        P.wait_all("sp", [dbuf(("out", t)) for t in range(nblk * 4)])
        with nc.Block() as block:
            P.emit(block)
        esA.close()
        return

    P.barrier()
    esA.close()

    esBC = ExitStack()
    kcT = esBC.enter_context(nc.sbuf_tensor("sbBC_kcT", [128, KC, LC], BF16))
    vc = esBC.enter_context(nc.sbuf_tensor("sbBC_vc", [128, 2, D], BF16))
    esB = ExitStack()

    def sbB(name, shape, dt=F32):
        return esB.enter_context(nc.sbuf_tensor("sbB_" + name, list(shape), dt))

    xinB = [sbB(f"xin{i}", [128, D]) for i in range(2)]
    xinB_b = [Buf(), Buf()]
    sqjB = sbB("sqj", [128, D], BF16)
    ssB = sbB("ss", [128, 8])
    rsB = sbB("rs", [128, 8])
    stB_b = [Buf() for i in range(8)]
    xnB = [sbB(f"xn{i}", [128, D], BF16) for i in range(4)]
    xnB_b = [Buf() for i in range(4)]
    hTB = [sbB(f"hT{i}", [128, KC, 512], BF16) for i in range(3)]
    hTB_b = [[Buf() for t in range(4)] for i in range(3)]
    fbB = (xinB, xinB_b, sqjB, ssB, rsB, stB_b, xnB, xnB_b, hTB, hTB_b)
    qsb = [sbB(f"qsb{i}", [128, 4, KC, 128], BF16) for i in range(2)]
    ksb = [sbB(f"ksb{i}", [128, 4, KC, 128], BF16) for i in range(2)]
    zsb = [sbB(f"zsb{i}", [128, 4, KC, 128], BF16) for i in range(2)]
    vsb0 = sbB("vsb0", [128, 4, D], BF16)
    vsb = [vsb0, vsb0]
    vsb0_b = Buf()
    qsb_b, ksb_b, zsb_b, vsb_b = [Buf(), Buf()], [Buf(), Buf()], [Buf(), Buf()], [vsb0_b, vsb0_b]
    stgB_t = [qsb[0], ksb[0], zsb[0], qsb[1], ksb[1], zsb[1]]
    stgB = [t_[:].rearrange("p a b c -> p (a b c)").bitcast(F32).rearrange("p (k n) -> p k n", k=KC) for t_ in stgB_t]
    load_weight_bf16(Win, Win_b, winn_d, 4 * D, stgB, [qsb_b[0], ksb_b[0], zsb_b[0], qsb_b[1], ksb_b[1], zsb_b[1]],
                     256)

    def x1buf(tok):
        return dbuf((id(x1_d), tok))

    def ctx1buf(tok):
        return dbuf((id(ctx1_d), tok))

    evr = [0]

    def evac_copy(out_ap, in_ap, reads, writes):
        eng = ("dve", "act")[evr[0] % 2]
        evr[0] += 1
        if eng == "act":
            P.op("act", lambda e: e.copy(out=out_ap, in_=in_ap), reads=reads, writes=writes)
        else:
            P.op("dve", lambda e: e.tensor_copy(out=out_ap, in_=in_ap), reads=reads, writes=writes)

    front_b(fbB, front_a(fbB, ctx1_d, 0, 2, src_bufs=ctx1buf), 0, 1, 1)
    xslotsB = {0: front_a(fbB, x1_d, 0, 4, src_bufs=x1buf)}
    front_b(fbB, xslotsB[0], 1, 1, 0)
    xslotsB[1] = front_a(fbB, x1_d, 512, 4, src_bufs=x1buf)
    for oc in range(KC):
        bank, bb = proj(hTB[0], hTB_b[0][:2], LC, D + oc * 128)
        evac_copy(kcT[:, oc, :], bank[:, 0:LC], [bb], [kcT_b])
    for tt in range(2):
        for h in range(2):
            bank, bb = next_pp()

            def mmv(e, bank=bank, tt=tt, h=h):
                ins = None
                for k in range(KC):
                    ins = e.matmul(bank[:, :], lhsT=hTB[0][:, k, tt * 128:(tt + 1) * 128],
                                   rhs=Win[:, k, 2 * D + h * 512: 2 * D + (h + 1) * 512],
                                   start=(k == 0), stop=(k == KC - 1))
                return ins

            P.op("pe", mmv, reads=[Win_b] + hTB_b[0][:2], writes=[bb])
            evac_copy(vc[:, tt, h * 512:(h + 1) * 512], bank[:, :], [bb], [vc_b])

    nblkB = NB if stop_after not in ("small", "AB_small") else 2
    scr_b = {}

    def sbuf_(key):
        if key not in scr_b:
            scr_b[key] = Buf(str(key))
        return scr_b[key]

    stB = {}
    for j in range(nblkB):
        slot = (j + 1) % 3
        ob = j % 2
        for oc in range(KC):
            if 2 <= oc <= 5 and j + 1 < nblkB:
                front_b(fbB, xslotsB[j + 1], (j + 2) % 3, 1, 0, only=oc - 2)
            if j + 2 < nblkB:
                if 2 <= oc <= 5:
                    stB[oc - 2] = fa_stat(fbB, x1_d, (j + 2) * 512 + (oc - 2) * 128, x1buf)
                if 3 <= oc <= 6:
                    xslotsB.setdefault(j + 2, []).append(fa_norm(fbB, stB[oc - 3]))
            bank, bb = proj(hTB[slot], hTB_b[slot], 512, oc * 128)
            P.op("act", lambda e, bank=bank, ob=ob, oc=oc: e.activation(
                out=qsb[ob][:, :, oc, :], in_=bank[:, :].rearrange("p (t n) -> p t n", t=4), func=AF.Copy,
                scale=0.125), reads=[bb], writes=[qsb_b[ob]])
            bank, bb = proj(hTB[slot], hTB_b[slot], 512, D + oc * 128)
            P.op("dve", lambda e, bank=bank, ob=ob, oc=oc: e.tensor_copy(
                out=ksb[ob][:, :, oc, :], in_=bank[:, :].rearrange("p (t n) -> p t n", t=4)),
                reads=[bb], writes=[ksb_b[ob]])
            bank, bb = proj(hTB[slot], hTB_b[slot], 512, 3 * D + oc * 128)
            P.op("act", lambda e, bank=bank, ob=ob, oc=oc: e.activation(
                out=zsb[ob][:, :, oc, :], in_=bank[:, :].rearrange("p (t n) -> p t n", t=4), func=AF.Silu),
                reads=[bb], writes=[zsb_b[ob]])
        for tt in range(4):
            for h in range(2):
                bank, bb = next_pp()

                def mmv(e, bank=bank, tt=tt, h=h, slot=slot):
                    ins = None
                    for k in range(KC):
                        ins = e.matmul(bank[:, :], lhsT=hTB[slot][:, k, tt * 128:(tt + 1) * 128],
                                       rhs=Win[:, k, 2 * D + h * 512: 2 * D + (h + 1) * 512],
                                       start=(k == 0), stop=(k == KC - 1))
                    return ins

                P.op("pe", mmv, reads=[Win_b] + hTB_b[slot], writes=[bb])
                evac_copy(vsb[ob][:, tt, h * 512:(h + 1) * 512], bank[:, :], [bb], [vsb_b[ob]])
        P.dma("sp", f"d_qo{ob}", qT_d[j], qsb[ob][:].rearrange("p a b c -> p (a b c)"), reads=[qsb_b[ob]],
              writes=[sbuf_(("q", j))])
        P.dma("sp", f"d_ko{ob}", kT_d[j], ksb[ob][:].rearrange("p a b c -> p (a b c)"), reads=[ksb_b[ob]],
              writes=[sbuf_(("k", j))])
        P.dma("sp", f"d_zo{ob}", szT_d[j], zsb[ob][:].rearrange("p a b c -> p (a b c)"), reads=[zsb_b[ob]],
              writes=[sbuf_(("z", j))])
        P.dma("sp", f"d_vo{ob}", v_d[j * 512:(j + 1) * 512, :].rearrange("(t p) d -> p t d", p=128), vsb[ob][:],
              reads=[vsb_b[ob]], writes=[sbuf_(("v", j))])

    P.barrier()
    if stop_after == "AB_small":
        with nc.Block() as block:
            P.emit(block)
        esB.close()
        esBC.close()
        return
    esB.close()

    esC = ExitStack()

    def sbC(name, shape, dt=F32):
        return esC.enter_context(nc.sbuf_tensor("sbC_" + name, list(shape), dt))

    NH = 16
    mtab = Win[:].rearrange("p k n -> p (k n)")[:, 0:NH * 24 * 64].rearrange("p (h x) -> p h x", h=NH)
    mtab_b = Buf("mtab")
    stgC = [sbC(f"stgC{i}", [128, 2048]) for i in range(2)]
    stgC_b = [Buf(), Buf()]
    load_weight_bf16(Wout, Wout_b, woutn_d, D,
                     [stgC[0][:].rearrange("p (k n) -> p k n", k=KC), stgC[1][:].rearrange("p (k n) -> p k n", k=KC)],
                     stgC_b, 256)
    for i in range(12):
        s_ = i % 2
        P.dma(("sp", "act")[i % 2], f"d_wst{s_}", stgC[s_][:], rpbG_d[:, i * 2048:(i + 1) * 2048],
              writes=[stgC_b[s_]])
        P.op("act", lambda e, s_=s_, i=i: e.activation(
            out=Win[:].rearrange("p k n -> p (k n)")[:, i * 2048:(i + 1) * 2048], in_=stgC[s_][:], func=AF.Exp),
            reads=[stgC_b[s_]], writes=[mtab_b])

    KR = 8
    kring = [sbC(f"kring{i}", [128, KC, 128], BF16) for i in range(KR)]
    vring = [sbC(f"vring{i}", [128, D], BF16) for i in range(KR)]
    kring_b = [Buf() for i in range(KR)]
    vring_b = [Buf() for i in range(KR)]
    qtl = [sbC(f"qtl{i}", [128, KC, 128], BF16) for i in range(2)]
    ztl = [sbC(f"ztl{i}", [128, KC, 128], BF16) for i in range(2)]
    x1t = [sbC(f"x1t{i}", [128, D]) for i in range(2)]
    qtl_b, ztl_b, x1t_b = [Buf(), Buf()], [Buf(), Buf()], [Buf(), Buf()]
    pT = [sbC(f"pT{i}", [128, 2, 896], BF16) for i in range(2)]
    pT_b = [Buf(), Buf()]
    yTC = [sbC(f"yTC{i}", [128, KC, 128], BF16) for i in range(2)]
    yTC_b = [[Buf() for c in range(KC)] for i in range(2)]
    rcC = [sbC(f"rcC{i}", [128, 128]) for i in range(2)]
    wC = [sbC(f"wC{i}", [128, 128]) for i in range(2)]
    rcC_b, wC_b = [Buf(), Buf()], [Buf(), Buf()]
    tmpC = [sbC(f"tmpC{i}", [128, D]) for i in range(2)]
    tmpC_b = [Buf(), Buf()]
    sqjC = sbC("sqjC", [128, 512], BF16)
    ss2C = sbC("ss2C", [128, 8, 2])
    rs2C = sbC("rs2C", [128, 8])
    st2C_b = [Buf() for i in range(8)]
    sS = psS[:].rearrange("p a n -> p (a n)")
    sS_b = [Buf("sA"), Buf("sB")]
    pOs = [pp4[:, 0:256], ptp[:].rearrange("p a b -> p (a b)").bitcast(F32)[:, 0:256]]
    pO_b = [Buf("pO0"), Buf("pO1")]

    NT = L // 128
    ntC = NT if stop_after != "small" else 6

    def rs_of(r):
        return min(max(r - 4, 0), 120)

    def keytiles(t):
        lo = rs_of(2 * t) // 2
        hi = (rs_of(2 * t + 1) + 7) // 2
        return list(range(hi, lo - 1, -1))

    def mask_off(t):
        kts = keytiles(t)
        if len(kts) == 5:
            return 0, 5
        dmax = kts[0] - t
        ti = 6 - 2 * dmax
        return (10 + ti) * 64, 4

    loaded = set()

    def ensure_keys(t):
        for kt in sorted(keytiles(t)):
            if kt in loaded:
                continue
            loaded.add(kt)
            sl = kt % KR
            P.dma("sp", f"d_kr{sl}", kring[sl][:].rearrange("p c n -> p (c n)"),
                  kT_d[kt // 4, :, (kt % 4) * 1024:(kt % 4 + 1) * 1024],
                  reads=[sbuf_(("k", kt // 4))], writes=[kring_b[sl]])
            P.dma("sp", f"d_vr{sl}", vring[sl][:], v_d[kt * 128:(kt + 1) * 128, :],
                  reads=[sbuf_(("v", kt // 4))], writes=[vring_b[sl]])

    def load_tile(t):
        s_ = t % 2
        P.dma("sp", f"d_ql{s_}", qtl[s_][:].rearrange("p c n -> p (c n)"),
              qT_d[t // 4, :, (t % 4) * 1024:(t % 4 + 1) * 1024], reads=[sbuf_(("q", t // 4))], writes=[qtl_b[s_]])
        P.dma("sp", f"d_zl{s_}", ztl[s_][:].rearrange("p c n -> p (c n)"),
              szT_d[t // 4, :, (t % 4) * 1024:(t % 4 + 1) * 1024], reads=[sbuf_(("z", t // 4))], writes=[ztl_b[s_]])
        P.dma("sp", f"d_xl{s_}", x1t[s_][:], x1_d[t * 128:(t + 1) * 128, :], reads=[x1buf(t * 128)],
              writes=[x1t_b[s_]])

    pTh_b = [[Buf(), Buf()], [Buf(), Buf()]]

    def s_exp(t, c, hh):
        ts_ = t % 2
        kts = keytiles(t)
        nk = len(kts)
        ncol = (2 + nk) * 128
        pb = c % 2
        p0 = hh * 64

        def mm(e):
            ins = None
            for s_i in range(2 + nk):
                if s_i < 2:
                    lhs = kcT[p0:p0 + 64, c, s_i * 128:(s_i + 1) * 128]
                else:
                    lhs = kring[kts[s_i - 2] % KR][p0:p0 + 64, c, :]
                ins = e.matmul(sS[:, hh * 1024 + s_i * 128: hh * 1024 + (s_i + 1) * 128], lhsT=lhs,
                               rhs=qtl[ts_][p0:p0 + 64, c, :], start=True, stop=True)
            return ins

        P.op("pe", mm, reads=[kcT_b, qtl_b[ts_]] + [kring_b[kt % KR] for kt in kts], writes=[sS_b[hh]])
        for (a_, b_) in ((0, 512), (512, ncol)):
            P.op("act", lambda e, a_=a_, b_=b_: e.activation(
                out=pT[pb][:, hh, a_:b_], in_=sS[:, hh * 1024 + a_: hh * 1024 + b_], func=AF.Exp),
                reads=[sS_b[hh]], writes=[pTh_b[pb][hh]])

    def maskmul(t, c, hh):
        kts = keytiles(t)
        nk = len(kts)
        ncol = (2 + nk) * 128
        pb = c % 2
        off, nk2 = mask_off(t)
        P.op("dve", lambda e: e.tensor_tensor(
            out=pT[pb][:, hh, 256:ncol], in0=pT[pb][:, hh, 256:ncol],
            in1=mtab[:, 2 * c + hh, off:off + nk * 128], op=ALU.mult),
            reads=[pTh_b[pb][hh], mtab_b], writes=[pTh_b[pb][hh]])

    def pv(t, c, hh):
        kts = keytiles(t)
        nk = len(kts)
        pb = c % 2
        ob = c % 2
        O = pOs[ob][:, 0:128]
        SM = pOs[ob][:, 128:256]
        p0 = hh * 64

        def mm(e):
            ins = None
            n = 2 + nk
            for s_i in range(n):
                if s_i < 2:
                    lhs = vc[:, s_i, c * 128 + p0: c * 128 + p0 + 64]
                else:
                    lhs = vring[kts[s_i - 2] % KR][:, c * 128 + p0: c * 128 + p0 + 64]
                rhs = pT[pb][:, hh, s_i * 128:(s_i + 1) * 128]
                e.matmul(O[p0:p0 + 64, :], lhsT=lhs, rhs=rhs, start=(s_i == 0), stop=(s_i == n - 1),
                         tile_position=(0, p0))
                ins = e.matmul(SM[p0:p0 + 64, :], lhsT=ones64[:, :], rhs=rhs, start=False,
                               stop=(s_i == n - 1), tile_position=(0, p0), skip_group_check=True)
            return ins

        P.op("pe", mm, reads=[vc_b, pTh_b[pb][hh], b_const] + [vring_b[kt % KR] for kt in kts], writes=[pO_b[ob]])

    def norm_items(t, c):
        ts_ = t % 2
        ob = c % 2
        O = pOs[ob][:, 0:128]
        SM = pOs[ob][:, 128:256]

        def r0():
            P.op("dve", lambda e: e.reciprocal(out=rcC[ob][:, 0:64], in_=SM[:, 0:64]), reads=[pO_b[ob]],
                 writes=[rcC_b[ob]])

        def r1():
            P.op("dve", lambda e: e.reciprocal(out=rcC[ob][:, 64:128], in_=SM[:, 64:128]), reads=[pO_b[ob]],
                 writes=[rcC_b[ob]])
            P.op("pool", lambda e: e.tensor_tensor(out=wC[ob][:], in0=rcC[ob][:], in1=ztl[ts_][:, c, :],
                                                   op=ALU.mult),
                 reads=[rcC_b[ob], ztl_b[ts_]], writes=[wC_b[ob]])

        def y_():
            P.op("dve", lambda e: e.tensor_tensor(out=yTC[ts_][:, c, :], in0=O, in1=wC[ob][:], op=ALU.mult),
                 reads=[pO_b[ob], wC_b[ob]], writes=[yTC_b[ts_][c]])
            if c == KC - 1:
                outproj(t)

        return [r0, r1, y_]

    def outproj(t):
        ts_ = t % 2
        si = t % 8
        for h in range(2):
            def mmo(e, h=h):
                ins = None
                for k in range(KC):
                    ins = e.matmul(py[:, h, :], lhsT=yTC[ts_][:, k, :], rhs=Wout[:, k, h * 512:(h + 1) * 512],
                                   start=(k == 0), stop=(k == KC - 1))
                return ins

            P.op("pe", mmo, reads=[Wout_b] + yTC_b[ts_], writes=[py_b[h]])
            P.op("act", lambda e, h=h: e.activation(out=sqjC[:], in_=py[:, h, :], func=AF.Square,
                                                    scale=1.0 / 32.0, accum_out=ss2C[:, si, h:h + 1]),
                 reads=[py_b[h]], writes=[st2C_b[si]])
        P.op("dve", lambda e: e.tensor_tensor(out=ss2C[:, si, 0:1], in0=ss2C[:, si, 0:1], in1=ss2C[:, si, 1:2],
                                              op=ALU.add), reads=[st2C_b[si]], writes=[st2C_b[si]])
        rstd_chain(ss2C[:, si, 0:1], rs2C[:, si:si + 1], st2C_b[si])
        def stt(h):
            P.op("dve", lambda e: e.scalar_tensor_tensor(
                out=tmpC[ts_][:, h * 512:(h + 1) * 512], in0=py[:, h, :], scalar=rs2C[:, si:si + 1],
                in1=GT[:, 2, h * 512:(h + 1) * 512], op0=ALU.mult, op1=ALU.mult),
                reads=[py_b[h], st2C_b[si], GT_b], writes=[tmpC_b[ts_]])

        def fin():
            for h in range(2):
                P.op("dve", lambda e, h=h: e.tensor_tensor(
                    out=tmpC[ts_][:, h * 512:(h + 1) * 512], in0=tmpC[ts_][:, h * 512:(h + 1) * 512],
                    in1=x1t[ts_][:, h * 512:(h + 1) * 512], op=ALU.add),
                    reads=[tmpC_b[ts_], x1t_b[ts_]], writes=[tmpC_b[ts_]])
            P.dma("sp", f"d_oo{ts_}", out_d[t * 128:(t + 1) * 128, :], tmpC[ts_][:], reads=[tmpC_b[ts_]],
                  writes=[dbuf(("out", t))])

        late.append((cur_step[0] + 4, lambda: stt(0)))
        late.append((cur_step[0] + 4, lambda: stt(1)))
        late.append((cur_step[0] + 5, fin))

    late = []
    cur_step = [0]
    items = [(t, c, hh) for t in range(ntC) for c in range(KC) for hh in range(2)]
    deferred = []
    ensure_keys(0)
    load_tile(0)
    LOOK = 2
    for n in range(min(LOOK, len(items))):
        s_exp(*items[n])
    for n, (t, c, hh) in enumerate(items):
        cur_step[0] = n
        if c == 4 and hh == 0 and t + 1 < ntC:
            ensure_keys(t + 1)
            load_tile(t + 1)
        if n + LOOK < len(items):
            s_exp(*items[n + LOOK])
        maskmul(t, c, hh)
        pv(t, c, hh)
        for _ in range(2):
            if deferred:
                deferred.pop(0)()
        if late and late[0][0] <= n:
            late.pop(0)[1]()
        if hh == 1:
            deferred.extend(norm_items(t, c))
    while deferred:
        deferred.pop(0)()
    while late:
        late.pop(0)[1]()
    P.wait_all("sp", [dbuf(("out", t)) for t in range(ntC)])
    P.barrier()
    with nc.Block() as block:
        P.emit(block)
    esC.close()
    esBC.close()


def _rpb_gather(rpb):
    NEG = np.float32(-30000.0)
    out = np.full((128, 16, 24, 64), NEG, dtype=np.float32)
    qc = np.arange(64)
    cs = np.clip(qc - 8, 0, 48)
    for kl in range(2):
        for kc in range(64):
            p = kl * 64 + kc
            valid_q = (cs <= kc) & (kc < cs + 16)
            dc = kc - qc + 15
            qv = qc[valid_q]
            for b in range(24):
                if b < 10:
                    dr = 4 - b + kl
                    if dr < -4 or dr > 3:
                        continue
                else:
                    dr = 6 - (b - 10) + kl
                    if dr < -7 or dr > 7:
                        continue
                out[p, :, b, qv] = rpb[:, dr + 7, dc[valid_q]].T
    return np.ascontiguousarray(out.reshape(128, 16 * 24 * 64))


def prep_inputs(inputs):
    f = lambda a: np.ascontiguousarray(np.asarray(a, dtype=np.float32))
    x, c, ctx, c_ctx = f(inputs["x"]), f(inputs["c"]), f(inputs["ctx"]), f(inputs["c_ctx"])
    g_pre, g_post, w_mod, b_mod = f(inputs["g_pre"]), f(inputs["g_post"]), f(inputs["w_mod"]), f(inputs["b_mod"])
    conv_w = f(inputs["conv_w"])
    shared = {
        "w_mod": w_mod,
        "bmodT": f(b_mod.reshape(2, 24, 128).transpose(2, 0, 1).reshape(128, 48)),
        "bgate": f(b_mod[:, 2 * D:3 * D]),
        "gpreT": f(g_pre.reshape(2, 8, 128).transpose(2, 0, 1).reshape(128, 16)),
        "gpost": g_post,
        "w_in_conv": f(inputs["w_in_conv"][0]),
        "w_out_conv": f(inputs["w_out_conv"][0]),
        "w_in_na": f(inputs["w_in_na"][0]),
        "w_out_na": f(inputs["w_out_na"][0]),
        "convwT": f(conv_w[0].reshape(3, 8, 128).transpose(2, 1, 0).reshape(128, 24)),
        "identd": np.eye(128, dtype=np.float32),
        "rpbG": _rpb_gather(f(inputs["rpb"][0])),
    }
    in_maps = []
    for b in range(NCORES):
        cv2 = np.stack([c[b], c_ctx], 0)
        m = dict(shared)
        m["x"] = x[b]
        m["ctx"] = ctx[b]
        m["cT"] = f(cv2.reshape(2, 8, 128).transpose(2, 0, 1).reshape(128, 16))
        in_maps.append(m)
    return in_maps


def kernel(**inputs):
    nc = build("all")
    in_maps = prep_inputs(inputs)
    res = run_bass_kernel_spmd(nc, in_maps, core_ids=list(range(NCORES)))
    return np.stack([r["out"] for r in res.results], 0)
```

```python
import os
import numpy as np
from contextlib import ExitStack
import concourse.bass as bass
import concourse.mybir as mybir
from concourse.bass_utils import run_bass_kernel_spmd

F32 = mybir.dt.float32
BF16 = mybir.dt.bfloat16
AF = mybir.ActivationFunctionType
ALU = mybir.AluOpType

D = 1024
L = 8192
LC = 256
NCORES = 8
EPS = 1e-6
KC = 8
NB = L // 512


class Buf:
    __slots__ = ("w", "r", "name")

    def __init__(self, name=""):
        self.w = None
        self.r = {}
        self.name = name


class Prog:
    ENG = ("pe", "act", "dve", "pool", "sp")

    def __init__(self, nc, es):
        self.nc = nc
        self.es = es
        self.q = {e: [] for e in self.ENG}
        self.cnt = {}
        self.sems = {}
        self.seen = {e: {} for e in self.ENG}
        for e in ("pe", "act", "dve", "pool"):
            self._sem("c_" + e)

    def _sem(self, key):
        if key not in self.sems:
            self.sems[key] = self.es.enter_context(self.nc.semaphore(key))
            self.cnt[key] = 0
        return self.sems[key]

    def _deps(self, eng, reads, writes):
        deps = []
        for b in reads:
            if b.w is not None:
                deps.append(b.w)
        for b in writes:
            if b.w is not None and b.w[2] != eng:
                deps.append(b.w)
            for t in b.r.values():
                if t[2] != eng:
                    deps.append(t)
        for (k, v, src) in deps:
            if src == "pe" and eng == "pe":
                continue
            if self.seen[eng].get(k, 0) >= v:
                continue
            self.seen[eng][k] = v
            self.q[eng].append(("wait", k, v))

    def _commit(self, tok, reads, writes):
        for b in reads:
            old = b.r.get(tok[0])
            if old is None or old[1] < tok[1]:
                b.r[tok[0]] = tok
        for b in writes:
            b.w = tok
            b.r = {}

    def op(self, eng, fn, reads=(), writes=()):
        self._deps(eng, reads, writes)
        key = "c_" + eng
        self.cnt[key] += 1
        tok = (key, self.cnt[key], eng)
        self.q[eng].append(("op", fn, key, 1))
        self._commit(tok, reads, writes)
        return tok

    def dma(self, qeng, semkey, out, in_, reads=(), writes=()):
        self._sem(semkey)
        self._deps(qeng, reads, writes)
        self.cnt[semkey] += 16
        tok = (semkey, self.cnt[semkey], "dma")
        self.q[qeng].append(("op", lambda e: e.dma_start(out=out, in_=in_), semkey, 16))
        self._commit(tok, reads, writes)
        return tok

    def barrier(self):
        for e in self.ENG:
            for k, v in self.cnt.items():
                if v > 0 and self.seen[e].get(k, 0) < v:
                    self.seen[e][k] = v
                    self.q[e].append(("wait", k, v))

    def wait_all(self, eng, bufs):
        self._deps(eng, bufs, ())

    def emit(self, block):
        def run(e, items):
            for it in items:
                if it[0] == "wait":
                    e.wait_ge(self.sems[it[1]], it[2])
                else:
                    ins = it[1](e)
                    ins.then_inc(self.sems[it[2]], it[3])

        block.tensor(lambda e: run(e, self.q["pe"]))
        block.scalar(lambda e: run(e, self.q["act"]))
        block.vector(lambda e: run(e, self.q["dve"]))
        block.gpsimd(lambda e: run(e, self.q["pool"]))
        block.sync(lambda e: run(e, self.q["sp"]))


def build(stop_after="all"):
    nc = bass.Bass("TRN2", target_bir_lowering=False)
    es = ExitStack()
    with es:
        _build(nc, es, stop_after)
    return nc


def _build(nc, es, stop_after):
    P = Prog(nc, es)

    def din(name, shape, dt=F32):
        return nc.dram_tensor(name, list(shape), dt, kind="ExternalInput").ap()

    x_d = din("x", [L, D])
    ctx_d = din("ctx", [LC, D])
    cT_d = din("cT", [128, 16])
    wmod_d = din("w_mod", [2, D, 3 * D])
    bmodT_d = din("bmodT", [128, 48])
    bgate_d = din("bgate", [2, D])
    gpreT_d = din("gpreT", [128, 16])
    gpost_d = din("gpost", [2, D])
    winc_d = din("w_in_conv", [D, 4 * D])
    woutc_d = din("w_out_conv", [D, D])
    winn_d = din("w_in_na", [D, 4 * D])
    woutn_d = din("w_out_na", [D, D])
    convwT_d = din("convwT", [128, 24])
    out_d = nc.dram_tensor("out", [L, D], F32, kind="ExternalOutput").ap()
    x1_d = nc.dram_tensor("x1_scr", [L, D], F32).ap()
    ctx1_d = nc.dram_tensor("ctx1_scr", [LC, D], F32).ap()
    qT_d = nc.dram_tensor("qT_scr", [NB, 128, 4 * KC * 128], BF16).ap()
    kT_d = nc.dram_tensor("kT_scr", [NB, 128, 4 * KC * 128], BF16).ap()
    szT_d = nc.dram_tensor("szT_scr", [NB, 128, 4 * KC * 128], BF16).ap()
    v_d = nc.dram_tensor("v_scr", [L, D], BF16).ap()
    rpbG_d = din("rpbG", [128, 16 * 24 * 64])

    def sb(name, shape, dt=F32):
        return es.enter_context(nc.sbuf_tensor("sb_" + name, list(shape), dt))

    def ps(name, shape, dt=F32):
        return es.enter_context(nc.psum_tensor("ps_" + name, list(shape), dt))

    ident = sb("ident", [128, 128], BF16)
    ones1 = sb("ones1", [1, 128], F32)
    cT = sb("cT", [128, 16])
    siluT = sb("siluT", [128, 16])
    bmodT = sb("bmodT_sb", [128, 48])
    gpreT = sb("gpreT_sb", [128, 16])
    convw = sb("convw_sb", [128, 24])
    Win = sb("Win", [128, KC, 4 * D], BF16)
    Wout = sb("Wout", [128, KC, D], BF16)
    Win_b, Wout_b = Buf("Win"), Buf("Wout")
    ones64 = sb("ones64", [128, 64], BF16)
    cst = sb("cst", [128, 2])
    kcT_b, vc_b = Buf("kcT"), Buf("vc")
    modT = sb("modT", [128, 2, 16, 2])
    Gs = sb("Gs", [128, 2, 2, 8])
    SHs = sb("SHs", [128, 2, 2, 8])
    GT = sb("GT", [128, 3, D])
    es_mod = ExitStack()

    def sbm(name, shape, dt=F32):
        return es_mod.enter_context(nc.sbuf_tensor("sm_" + name, list(shape), dt))

    identf = sbm("identf", [128, 128], F32)
    silu_bc = sbm("silu_bc", [128, 16, 128])
    bgate = sbm("bgate_sb", [1, 2 * D])
    gpost_bc = sbm("gpost_bc", [128, 2, D])

    b_const = Buf("const")

    psS = ps("psS", [128, 4, 512])
    pp4 = ps("pp4", [128, 512])
    pp = [psS[:, i, :] for i in range(4)] + [pp4[:, :]]
    pp_b = [Buf(f"pp{i}") for i in range(5)]
    py = ps("py", [128, 2, 512])
    py_b = [Buf("py0"), Buf("py1")]
    ptp = ps("ptp", [128, 8, 128], BF16)
    ptp_b = Buf("ptp")
    pp_rr = [0]

    def next_pp():
        i = pp_rr[0] % 4
        pp_rr[0] += 1
        return pp[i], pp_b[i]

    ptps = [ptp[:], pp4[:].bitcast(BF16).rearrange("p (k n) -> p k n", k=KC)]
    ptps_b = [ptp_b, pp_b[4]]
    tp_rr = [0]

    def mk_ident(e):
        e.memset(ones64[:], 1.0)
        e.memset(cst[:, 0:1], float(EPS))
        e.memset(cst[:, 1:2], -0.5)
        return e.memset(ones1[:], 1.0)

    P.op("pool", mk_ident, writes=[b_const])
    ld_b = Buf("ld_small")
    P.dma("sp", "d_small", cT[:], cT_d, writes=[ld_b])
    P.dma("sp", "d_small", bmodT[:], bmodT_d, writes=[ld_b])
    P.dma("sp", "d_small", gpreT[:], gpreT_d, writes=[ld_b])
    P.dma("sp", "d_small", convw[:], convwT_d, writes=[ld_b])
    P.dma("sp", "d_small", bgate[:], bgate_d.rearrange("(o a) d -> o (a d)", o=1), writes=[ld_b])
    P.dma("sp", "d_small", gpost_bc[:].rearrange("p a d -> p (a d)"),
          gpost_d.rearrange("(o a) d -> o (a d)", o=1).partition_broadcast(128), writes=[ld_b])
    identd = din("identd", [128, 128])
    P.dma("sp", "d_small", identf[:], identd, reads=[b_const], writes=[ld_b])

    siluT_b = Buf("siluT")
    P.op("act", lambda e: e.activation(out=siluT[:], in_=cT[:], func=AF.Silu), reads=[ld_b], writes=[siluT_b])
    ident_b = Buf("ident")
    P.op("dve", lambda e: e.tensor_copy(out=ident[:], in_=identf[:]), reads=[ld_b], writes=[ident_b])
    silu_bc_b = Buf("silu_bc")
    P.op("dve", lambda e: e.tensor_copy(out=silu_bc[:], in_=siluT[:].unsqueeze(2).to_broadcast([128, 16, 128])),
         reads=[siluT_b], writes=[silu_bc_b])

    NWST = 4
    wst = [sbm(f"wst{i}", [128, KC, 512], F32) for i in range(NWST)]
    wst_b = [Buf(f"wst{i}") for i in range(NWST)]
    modT_b = Buf("modT")
    GT_b = Buf("GT")
    cvt_rr = [0]
    ci_holder = [0]
    wtag = [0]

    def weight_chunks(dst, dst_b, src_d, ncols, stg, stg_b, wcol):
        out = []
        wtag[0] += 1
        tag = wtag[0]
        for c in range(ncols // wcol):
            def emit(c=c):
                s = ci_holder[0] % len(stg)
                ci_holder[0] += 1
                P.dma(("sp", "act")[ci_holder[0] % 2], f"d_wl{tag % 2}_{s}", stg[s],
                      src_d[:, c * wcol:(c + 1) * wcol].rearrange("(k p) n -> p k n", p=128), writes=[stg_b[s]])
                eng = ("dve", "act")[cvt_rr[0] % 2]
                cvt_rr[0] += 1
                if eng == "act":
                    P.op("act", lambda e: e.copy(out=dst[:, :, c * wcol:(c + 1) * wcol], in_=stg[s]),
                         reads=[stg_b[s]], writes=[dst_b])
                else:
                    P.op("dve", lambda e: e.tensor_copy(out=dst[:, :, c * wcol:(c + 1) * wcol], in_=stg[s]),
                         reads=[stg_b[s]], writes=[dst_b])
            out.append(emit)
        return out

    def load_weight_bf16(dst, dst_b, src_d, ncols, stg, stg_b, wcol):
        for f_ in weight_chunks(dst, dst_b, src_d, ncols, stg, stg_b, wcol):
            f_()

    stgA = [wst[2][:], wst[3][:]]
    stgA_b = [wst_b[2], wst_b[3]]
    pendingA = (weight_chunks(Win, Win_b, winc_d, 4 * D, stgA, stgA_b, 512)
                + weight_chunks(Wout, Wout_b, woutc_d, D, stgA, stgA_b, 512))
    ci = 0
    for layer in range(2):
        for nch in range(6):
            for _ in range(4):
                if pendingA:
                    pendingA.pop(0)()
            s = ci % 2
            ci += 1
            P.dma(("sp", "act")[ci % 2], f"d_wst{s}", wst[s][:],
                  wmod_d[layer, :, nch * 512:(nch + 1) * 512].rearrange("(k p) n -> p k n", p=128),
                  writes=[wst_b[s]])
            if nch < 4:
                bank, bb = next_pp()

                def mm(e, s=s, bank=bank):
                    ins = None
                    for m in range(4):
                        for k in range(KC):
                            ins = e.matmul(bank[:, m * 2:m * 2 + 2], lhsT=wst[s][:, k, m * 128:(m + 1) * 128],
                                           rhs=siluT[:].rearrange("p (w k) -> p k w", w=2)[:, k, :],
                                           start=(k == 0), stop=(k == KC - 1))
                    return ins

                P.op("pe", mm, reads=[wst_b[s], siluT_b], writes=[bb])
                P.op("dve", lambda e, bank=bank, layer=layer, nch=nch: e.tensor_copy(
                    out=modT[:, layer, nch * 4:(nch + 1) * 4, :],
                    in_=bank[:, 0:8].rearrange("p (m w) -> p m w", w=2)),
                    reads=[bb], writes=[modT_b])
            else:
                half = nch - 4
                for w in range(2):
                    if layer == 1 and w == 1:
                        continue
                    gi = 0 if (layer == 0 and w == 0) else (1 if layer == 0 else 2)
                    bank, bb = next_pp()

                    def mmg(e, s=s, bank=bank, w=w, layer=layer, half=half):
                        for k in range(KC):
                            e.matmul(bank[:, :], lhsT=silu_bc[:, w * 8 + k, :], rhs=wst[s][:, k, :],
                                     start=(k == 0), stop=False)
                        return e.matmul(bank[:, :], lhsT=ones1[:, :],
                                        rhs=bgate[:, layer * D + half * 512: layer * D + (half + 1) * 512],
                                        start=False, stop=True)

                    P.op("pe", mmg, reads=[wst_b[s], silu_bc_b, ld_b, b_const], writes=[bb])
                    P.op("dve", lambda e, bank=bank, gi=gi, layer=layer, half=half: e.scalar_tensor_tensor(
                        out=GT[:, gi, half * 512:(half + 1) * 512], in0=bank[:, :], scalar=1.0,
                        in1=gpost_bc[:, layer, half * 512:(half + 1) * 512], op0=ALU.mult, op1=ALU.mult),
                        reads=[bb, ld_b], writes=[GT_b])
    GS_b = Buf("GS")
    for layer in range(2):
        for w in range(2):
            P.op("dve", lambda e, layer=layer, w=w: e.tensor_tensor(
                out=SHs[:, layer, w, :], in0=modT[:, layer, 0:8, w], in1=bmodT[:, layer * 24:layer * 24 + 8],
                op=ALU.add), reads=[modT_b, ld_b], writes=[GS_b])
            P.op("dve", lambda e, layer=layer, w=w: e.scalar_tensor_tensor(
                out=Gs[:, layer, w, :], in0=modT[:, layer, 8:16, w], scalar=1.0,
                in1=bmodT[:, layer * 24 + 8:layer * 24 + 16], op0=ALU.add, op1=ALU.add),
                reads=[modT_b, ld_b], writes=[GS_b])
            P.op("dve", lambda e, layer=layer, w=w: e.scalar_tensor_tensor(
                out=Gs[:, layer, w, :], in0=Gs[:, layer, w, :], scalar=1.0,
                in1=gpreT[:, layer * 8:(layer + 1) * 8], op0=ALU.mult, op1=ALU.mult),
                reads=[GS_b, ld_b], writes=[GS_b])

    ci_holder = [ci]

    while pendingA:
        pendingA.pop(0)()
    P.barrier()
    es_mod.close()

    esA = ExitStack()

    def sbA(name, shape, dt=F32):
        return esA.enter_context(nc.sbuf_tensor("sa_" + name, list(shape), dt))

    NX = 2
    xin = [sbA(f"xin{i}", [128, D]) for i in range(NX)]
    xin_b = [Buf(f"xin{i}") for i in range(NX)]
    sqj = sbA("sqj", [128, D], BF16)
    ss = sbA("ss", [128, 8])
    rs = sbA("rs", [128, 8])
    st_b = [Buf(f"st{i}") for i in range(8)]
    xn = [sbA(f"xn{i}", [128, D], BF16) for i in range(4)]
    xn_b = [Buf(f"xn{i}") for i in range(4)]
    hT = [sbA(f"hT{i}", [128, KC, 512], BF16) for i in range(3)]
    hT_b = [[Buf(f"hT{i}_{t}") for t in range(4)] for i in range(3)]
    cu = [sbA(f"cu{i}", [128, KC, 514]) for i in range(2)]
    cu_b = [[Buf(f"cu{i}_{c}") for c in range(KC)] for i in range(2)]
    hl_b = [Buf("hl0"), Buf("hl1")]
    hr_b = [Buf("hr0"), Buf("hr1")]
    cg = [sbA(f"cg{i}", [128, 512]) for i in range(2)]
    cg_b = [Buf("cg0"), Buf("cg1")]
    szb = [sbA(f"sz{i}", [128, 512]) for i in range(2)]
    sz_b = [Buf("sz0"), Buf("sz1")]
    cv = [sbA(f"cv{i}", [128, 512]) for i in range(2)]
    cv_b = [Buf("cv0"), Buf("cv1")]
    yT = sbA("yT", [128, KC, 512], BF16)
    yT_b = [Buf(f"yT{c}") for c in range(KC)]
    tmp = [sbA(f"tmp{i}", [128, D]) for i in range(2)]
    tmp_b = [Buf("tmp0"), Buf("tmp1")]
    xres = [sbA(f"xres{i}", [128, D]) for i in range(NX)]
    xres_b = [Buf(f"xres{i}") for i in range(NX)]
    ss2 = sbA("ss2", [128, 8, 2])
    rs2 = sbA("rs2", [128, 8])
    st2_b = [Buf(f"st2{i}") for i in range(8)]

    cnt = {"tile": 0, "chunk": 0, "otile": 0}
    fbA = (xin, xin_b, sqj, ss, rs, st_b, xn, xn_b, hT, hT_b)

    def rstd_chain(ms_ap, rs_ap, buf):
        P.op("pool", lambda e: e.tensor_tensor(out=ms_ap, in0=ms_ap, in1=cst[:, 0:1], op=ALU.add),
             reads=[buf, b_const], writes=[buf])
        P.op("pool", lambda e: e.tensor_tensor(out=rs_ap, in0=ms_ap, in1=cst[:, 1:2], op=ALU.pow),
             reads=[buf, b_const], writes=[buf])

    NXN = 4

    def fa_load(fb, src_d, tok, src_bufs=None):
        xin, xin_b, sqj, ss, rs, st_b, xn, xn_b, hT, hT_b = fb
        i = cnt["tile"]
        cnt["tile"] += 1
        xs = i % NX
        P.dma("sp", f"d_xin{xs}", xin[xs][:], src_d[tok: tok + 128, :],
              reads=([] if src_bufs is None else [src_bufs(tok)]), writes=[xin_b[xs]])
        return (xs, i % 8, i % NXN)

    def fa_stat(fb, desc):
        xin, xin_b, sqj, ss, rs, st_b, xn, xn_b, hT, hT_b = fb
        xs, si, xb = desc
        P.op("act", lambda e: e.activation(out=sqj[:], in_=xin[xs][:], func=AF.Square,
                                           scale=1.0 / 32.0, accum_out=ss[:, si:si + 1]),
             reads=[xin_b[xs]], writes=[st_b[si]])
        rstd_chain(ss[:, si:si + 1], rs[:, si:si + 1], st_b[si])

    def fa_norm(fb, desc):
        xin, xin_b, sqj, ss, rs, st_b, xn, xn_b, hT, hT_b = fb
        xs, si, xb = desc
        P.op("act", lambda e: e.activation(out=xn[xb][:], in_=xin[xs][:], func=AF.Copy, scale=rs[:, si:si + 1]),
             reads=[xin_b[xs], st_b[si]], writes=[xn_b[xb]])
        return xb

    def fa_steps(fb, src_d, tok0, ntile, out_slots, src_bufs=None):
        dsc = {}

        def step(s_):
            if 0 <= s_ - 2 < ntile:
                out_slots.append(fa_norm(fb, dsc[s_ - 2]))
            if 0 <= s_ < ntile:
                dsc[s_] = fa_load(fb, src_d, tok0 + s_ * 128, src_bufs)
            if 0 <= s_ - 1 < ntile:
                fa_stat(fb, dsc[s_ - 1])

        return [lambda s_=s_: step(s_) for s_ in range(ntile + 2)]

    def front_a(fb, src_d, tok0, ntile, src_bufs=None):
        slots = []
        for f_ in fa_steps(fb, src_d, tok0, ntile, slots, src_bufs):
            f_()
        return slots

    def front_b(fb, slots, slot, layer, w, only=None):
        xin, xin_b, sqj, ss, rs, st_b, xn, xn_b, hT, hT_b = fb
        for tt, xb in enumerate(slots):
            if only is not None and tt != only:
                continue
            pi = tp_rr[0] % 2
            tp_rr[0] += 1
            tpb = ptps[pi]
            tpb_b = ptps_b[pi]

            def tps(e, xb=xb, tpb=tpb):
                ins = None
                for k in range(KC):
                    ins = e.transpose(tpb[:, k, :], xn[xb][:, k * 128:(k + 1) * 128], ident[:])
                return ins

            P.op("pe", tps, reads=[xn_b[xb], ident_b], writes=[tpb_b])
            for k in range(KC):
                P.op("dve", lambda e, k=k, tt=tt, tpb=tpb: e.tensor_scalar(
                    out=hT[slot][:, k, tt * 128:(tt + 1) * 128], in0=tpb[:, k, :],
                    scalar1=Gs[:, layer, w, k:k + 1], scalar2=SHs[:, layer, w, k:k + 1],
                    op0=ALU.mult, op1=ALU.add),
                    reads=[tpb_b, GS_b], writes=[hT_b[slot][tt]])

    def proj(hT_t, hT_bl, nt, col0):
        bank, bb = next_pp()

        def mm(e):
            ins = None
            for k in range(KC):
                ins = e.matmul(bank[:, 0:nt], lhsT=Win[:, k, col0:col0 + 128], rhs=hT_t[:, k, 0:nt],
                               start=(k == 0), stop=(k == KC - 1))
            return ins

        P.op("pe", mm, reads=[Win_b] + list(hT_bl), writes=[bb])
        return bank, bb

    blocks = [(ctx_d, ctx1_d, 0, 2, 1, 1, True, True)]
    for j in range(NB):
        blocks.append((x_d, x1_d, j * 512, 4, 0, 0, j == 0, j == NB - 1))
    if stop_after in ("A_small", "small", "AB_small"):
        blocks = blocks[:3]
        blocks[-1] = blocks[-1][:7] + (True,)

    x1_b = {}
    def dbuf(key):
        if key not in x1_b:
            x1_b[key] = Buf(str(key))
        return x1_b[key]

    def mm1(bi, hooks=None):
        src, dst, tok0, ntile, w, gi, first, last = blocks[bi]
        slot = bi % 2
        hs = bi % 3
        nt = ntile * 128
        for oc in range(KC):
            i = cnt["chunk"]
            cnt["chunk"] += 1
            cs = i % 2
            if hooks and oc in hooks:
                for f_ in hooks[oc]:
                    f_()
            bank_c, bb_c = proj(hT[hs], hT_b[hs][:ntile], nt, D + oc * 128)
            bank_u, bb_u = proj(hT[hs], hT_b[hs][:ntile], nt, 2 * D + oc * 128)
            P.op("act", lambda e, bank_c=bank_c, cs=cs: e.copy(out=cg[cs][:, 0:nt], in_=bank_c[:, 0:nt]),
                 reads=[bb_c], writes=[cg_b[cs]])
            P.op("dve", lambda e, bank_u=bank_u, cs=cs, oc=oc: e.tensor_tensor(
                out=cu[slot][:, oc, 1:nt + 1], in0=bank_u[:, 0:nt], in1=cg[cs][:, 0:nt], op=ALU.mult),
                reads=[bb_u, cg_b[cs]], writes=[cu_b[slot][oc]])
        if first:
            P.op("pool", lambda e: e.memset(cu[slot][:, :, 0:1], 0.0), writes=[hl_b[slot]])
        else:
            ps_ = (bi - 1) % 2
            pnt = blocks[bi - 1][3] * 128
            P.op("pool", lambda e: e.tensor_copy(out=cu[slot][:, :, 0:1], in_=cu[ps_][:, :, pnt:pnt + 1]),
                 reads=cu_b[ps_], writes=[hl_b[slot]])
            P.op("pool", lambda e: e.tensor_copy(out=cu[ps_][:, :, pnt + 1:pnt + 2], in_=cu[slot][:, :, 1:2]),
                 reads=cu_b[slot], writes=[hr_b[ps_]])
        if last:
            P.op("pool", lambda e: e.memset(cu[slot][:, :, nt + 1:nt + 2], 0.0), writes=[hr_b[slot]])

    def back(bi):
        src, dst, tok0, ntile, w, gi, first, last = blocks[bi]
        slot = bi % 2
        hs = bi % 3
        nt = ntile * 128
        for oc in range(KC):
            i = cnt["chunk"]
            cnt["chunk"] += 1
            cs = i % 2
            P.op("act", lambda e, cs=cs, oc=oc: e.activation(
                out=cv[cs][:, 0:nt], in_=cu[slot][:, oc, 1:nt + 1], func=AF.Copy,
                scale=convw[:, oc * 3 + 1:oc * 3 + 2]),
                reads=[cu_b[slot][oc], ld_b], writes=[cv_b[cs]])
            P.op("dve", lambda e, cs=cs, oc=oc: e.scalar_tensor_tensor(
                out=cv[cs][:, 0:nt], in0=cu[slot][:, oc, 0:nt], scalar=convw[:, oc * 3:oc * 3 + 1],
                in1=cv[cs][:, 0:nt], op0=ALU.mult, op1=ALU.add),
                reads=[cu_b[slot][oc], hl_b[slot], cv_b[cs]], writes=[cv_b[cs]])
            P.op("dve", lambda e, cs=cs, oc=oc: e.scalar_tensor_tensor(
                out=cv[cs][:, 0:nt], in0=cu[slot][:, oc, 2:nt + 2], scalar=convw[:, oc * 3 + 2:oc * 3 + 3],
                in1=cv[cs][:, 0:nt], op0=ALU.mult, op1=ALU.add),
                reads=[cu_b[slot][oc], hr_b[slot], cv_b[cs]], writes=[cv_b[cs]])
            bank_b, bb_b = proj(hT[hs], hT_b[hs][:ntile], nt, oc * 128)
            bank_z, bb_z = proj(hT[hs], hT_b[hs][:ntile], nt, 3 * D + oc * 128)
            P.op("act", lambda e, bank_z=bank_z, cs=cs: e.activation(out=szb[cs][:, 0:nt], in_=bank_z[:, 0:nt],
                                                                      func=AF.Silu),
                 reads=[bb_z], writes=[sz_b[cs]])
            P.op("dve", lambda e, bank_b=bank_b, cs=cs: e.tensor_tensor(
                out=cv[cs][:, 0:nt], in0=bank_b[:, 0:nt], in1=cv[cs][:, 0:nt], op=ALU.mult),
                reads=[bb_b, cv_b[cs]], writes=[cv_b[cs]])
            P.op("dve", lambda e, cs=cs, oc=oc: e.tensor_tensor(
                out=yT[:, oc, 0:nt], in0=cv[cs][:, 0:nt], in1=szb[cs][:, 0:nt], op=ALU.mult),
                reads=[cv_b[cs], sz_b[cs]], writes=[yT_b[oc]])

    def outproj_tile(bi, tt):
        src, dst, tok0, ntile, w, gi, first, last = blocks[bi]
        if tt >= ntile:
            return
        if True:
            i = cnt["otile"]
            cnt["otile"] += 1
            xs = i % NX
            si = i % 8
            ts_ = i % 2
            P.dma("sp", f"d_xres{xs}", xres[xs][:], src[tok0 + tt * 128: tok0 + (tt + 1) * 128, :],
                  writes=[xres_b[xs]])
            for h in range(2):
                def mmo(e, h=h, tt=tt):
                    ins = None
                    for k in range(KC):
                        ins = e.matmul(py[:, h, :], lhsT=yT[:, k, tt * 128:(tt + 1) * 128],
                                       rhs=Wout[:, k, h * 512:(h + 1) * 512], start=(k == 0), stop=(k == KC - 1))
                    return ins

                P.op("pe", mmo, reads=[Wout_b] + yT_b, writes=[py_b[h]])
                P.op("act", lambda e, h=h, si=si: e.activation(out=sqj[:, 0:512], in_=py[:, h, :], func=AF.Square,
                                                               scale=1.0 / 32.0, accum_out=ss2[:, si, h:h + 1]),
                     reads=[py_b[h]], writes=[st2_b[si]])
            P.op("dve", lambda e, si=si: e.tensor_tensor(out=ss2[:, si, 0:1], in0=ss2[:, si, 0:1],
                                                         in1=ss2[:, si, 1:2], op=ALU.add),
                 reads=[st2_b[si]], writes=[st2_b[si]])
            rstd_chain(ss2[:, si, 0:1], rs2[:, si:si + 1], st2_b[si])
            for h in range(2):
                P.op("dve", lambda e, h=h, si=si, ts_=ts_: e.scalar_tensor_tensor(
                    out=tmp[ts_][:, h * 512:(h + 1) * 512], in0=py[:, h, :], scalar=rs2[:, si:si + 1],
                    in1=GT[:, gi, h * 512:(h + 1) * 512], op0=ALU.mult, op1=ALU.mult),
                    reads=[py_b[h], st2_b[si], GT_b], writes=[tmp_b[ts_]])
            for h in range(2):
                P.op("dve", lambda e, ts_=ts_, xs=xs, h=h: e.tensor_tensor(
                    out=tmp[ts_][:, h * 512:(h + 1) * 512], in0=tmp[ts_][:, h * 512:(h + 1) * 512],
                    in1=xres[xs][:, h * 512:(h + 1) * 512], op=ALU.add),
                    reads=[tmp_b[ts_], xres_b[xs]], writes=[tmp_b[ts_]])
            P.dma("sp", f"d_st{ts_}", dst[tok0 + tt * 128: tok0 + (tt + 1) * 128, :], tmp[ts_][:],
                  reads=[tmp_b[ts_]], writes=[dbuf((id(dst), tok0 + tt * 128))])

    nbl = len(blocks)
    xslots = {}

    def fa(b):
        if b < nbl:
            xslots[b] = front_a(fbA, blocks[b][0], blocks[b][2], blocks[b][3])

    def fbt(b, only=None):
        if b < nbl:
            front_b(fbA, xslots[b], b % 3, 0, blocks[b][4], only=only)

    fa(0)
    fbt(0)
    fa(1)
    mm1(0)
    fbt(1)
    fa(2)
    for i in range(nbl + 1):
        if i >= 1:
            back(i - 1)
        if i + 1 < nbl:
            hooks = {}
            if i >= 1:
                for k_, oc_ in enumerate((1, 3, 5, 7)):
                    hooks.setdefault(oc_, []).append(lambda k_=k_, i=i: outproj_tile(i - 1, k_))
            for tt_, oc_ in enumerate((2, 3, 4, 5)):
                hooks.setdefault(oc_, []).append(lambda i=i, tt_=tt_: fbt(i + 2, only=tt_))
            if i + 3 < nbl:
                b3 = i + 3
                xslots[b3] = []
                steps_ = fa_steps(fbA, blocks[b3][0], blocks[b3][2], blocks[b3][3], xslots[b3])
                for k_, f_ in enumerate(steps_):
                    hooks.setdefault(1 + k_, []).append(f_)
            mm1(i + 1, hooks)
        elif i >= 1:
            for k_ in range(4):
                outproj_tile(i - 1, k_)

    if stop_after in ("A", "A_small"):
        nblk = len(blocks) - 1
        for t in range(nblk * 4):
            s = t % 2
            P.dma("sp", f"d_dbg{s}", tmp[s][:], x1_d[t * 128:(t + 1) * 128, :],
                  reads=[dbuf((id(x1_d), t * 128))], writes=[tmp_b[s]])
            P.dma("sp", f"d_dbo{s}", out_d[t * 128:(t + 1) * 128, :], tmp[s][:], reads=[tmp_b[s]],
                  writes=[dbuf(("out", t))])
        P.wait_all("sp", [dbuf(("out", t)) for t in range(nblk * 4)])
        with nc.Block() as block:
            P.emit(block)
        esA.close()
        return

    P.barrier()
    esA.close()

    esBC = ExitStack()
    kcT = esBC.enter_context(nc.sbuf_tensor("sbBC_kcT", [128, KC, LC], BF16))
    vc = esBC.enter_context(nc.sbuf_tensor("sbBC_vc", [128, 2, D], BF16))
    esB = ExitStack()

    def sbB(name, shape, dt=F32):
        return esB.enter_context(nc.sbuf_tensor("sbB_" + name, list(shape), dt))

    xinB = [sbB(f"xin{i}", [128, D]) for i in range(2)]
    xinB_b = [Buf(), Buf()]
    sqjB = sbB("sqj", [128, D], BF16)
    ssB = sbB("ss", [128, 8])
    rsB = sbB("rs", [128, 8])
    stB_b = [Buf() for i in range(8)]
    xnB = [sbB(f"xn{i}", [128, D], BF16) for i in range(4)]
    xnB_b = [Buf() for i in range(4)]
    hTB = [sbB(f"hT{i}", [128, KC, 512], BF16) for i in range(3)]
    hTB_b = [[Buf() for t in range(4)] for i in range(3)]
    fbB = (xinB, xinB_b, sqjB, ssB, rsB, stB_b, xnB, xnB_b, hTB, hTB_b)
    qsb = [sbB(f"qsb{i}", [128, 4, KC, 128], BF16) for i in range(2)]
    ksb = [sbB(f"ksb{i}", [128, 4, KC, 128], BF16) for i in range(2)]
    zsb = [sbB(f"zsb{i}", [128, 4, KC, 128], BF16) for i in range(2)]
    vsb0 = sbB("vsb0", [128, 4, D], BF16)
    vsb = [vsb0, vsb0]
    vsb0_b = Buf()
    qsb_b, ksb_b, zsb_b, vsb_b = [Buf(), Buf()], [Buf(), Buf()], [Buf(), Buf()], [vsb0_b, vsb0_b]
    stgB_t = [qsb[0], ksb[0], zsb[0], qsb[1], ksb[1], zsb[1]]
    stgB = [t_[:].rearrange("p a b c -> p (a b c)").bitcast(F32).rearrange("p (k n) -> p k n", k=KC) for t_ in stgB_t]
    load_weight_bf16(Win, Win_b, winn_d, 4 * D, stgB, [qsb_b[0], ksb_b[0], zsb_b[0], qsb_b[1], ksb_b[1], zsb_b[1]],
                     256)

    def x1buf(tok):
        return dbuf((id(x1_d), tok))

    def ctx1buf(tok):
        return dbuf((id(ctx1_d), tok))

    evr = [0]

    def evac_copy(out_ap, in_ap, reads, writes):
        eng = ("dve", "act")[evr[0] % 2]
        evr[0] += 1
        if eng == "act":
            P.op("act", lambda e: e.copy(out=out_ap, in_=in_ap), reads=reads, writes=writes)
        else:
            P.op("dve", lambda e: e.tensor_copy(out=out_ap, in_=in_ap), reads=reads, writes=writes)

    front_b(fbB, front_a(fbB, ctx1_d, 0, 2, src_bufs=ctx1buf), 0, 1, 1)
    xslotsB = {0: front_a(fbB, x1_d, 0, 4, src_bufs=x1buf)}
    front_b(fbB, xslotsB[0], 1, 1, 0)
    xslotsB[1] = front_a(fbB, x1_d, 512, 4, src_bufs=x1buf)
    for oc in range(KC):
        bank, bb = proj(hTB[0], hTB_b[0][:2], LC, D + oc * 128)
        evac_copy(kcT[:, oc, :], bank[:, 0:LC], [bb], [kcT_b])
    for tt in range(2):
        for h in range(2):
            bank, bb = next_pp()

            def mmv(e, bank=bank, tt=tt, h=h):
                ins = None
                for k in range(KC):
                    ins = e.matmul(bank[:, :], lhsT=hTB[0][:, k, tt * 128:(tt + 1) * 128],
                                   rhs=Win[:, k, 2 * D + h * 512: 2 * D + (h + 1) * 512],
                                   start=(k == 0), stop=(k == KC - 1))
                return ins

            P.op("pe", mmv, reads=[Win_b] + hTB_b[0][:2], writes=[bb])
            evac_copy(vc[:, tt, h * 512:(h + 1) * 512], bank[:, :], [bb], [vc_b])

    nblkB = NB if stop_after not in ("small", "AB_small") else 2
    scr_b = {}

    def sbuf_(key):
        if key not in scr_b:
            scr_b[key] = Buf(str(key))
        return scr_b[key]

    stB = {}
    for j in range(nblkB):
        slot = (j + 1) % 3
        ob = j % 2
        for oc in range(KC):
            if 2 <= oc <= 5 and j + 1 < nblkB:
                front_b(fbB, xslotsB[j + 1], (j + 2) % 3, 1, 0, only=oc - 2)
            if j + 2 < nblkB:
                if oc == 1:
                    xslotsB[j + 2] = []
                    stB["steps"] = fa_steps(fbB, x1_d, (j + 2) * 512, 4, xslotsB[j + 2], x1buf)
                if 1 <= oc <= 6:
                    stB["steps"][oc - 1]()
            bank, bb = proj(hTB[slot], hTB_b[slot], 512, oc * 128)
            P.op("act", lambda e, bank=bank, ob=ob, oc=oc: e.activation(
                out=qsb[ob][:, :, oc, :], in_=bank[:, :].rearrange("p (t n) -> p t n", t=4), func=AF.Copy,
                scale=0.125), reads=[bb], writes=[qsb_b[ob]])
            bank, bb = proj(hTB[slot], hTB_b[slot], 512, D + oc * 128)
            P.op("dve", lambda e, bank=bank, ob=ob, oc=oc: e.tensor_copy(
                out=ksb[ob][:, :, oc, :], in_=bank[:, :].rearrange("p (t n) -> p t n", t=4)),
                reads=[bb], writes=[ksb_b[ob]])
            bank, bb = proj(hTB[slot], hTB_b[slot], 512, 3 * D + oc * 128)
            P.op("act", lambda e, bank=bank, ob=ob, oc=oc: e.activation(
                out=zsb[ob][:, :, oc, :], in_=bank[:, :].rearrange("p (t n) -> p t n", t=4), func=AF.Silu),
                reads=[bb], writes=[zsb_b[ob]])
        for tt in range(4):
            for h in range(2):
                bank, bb = next_pp()

                def mmv(e, bank=bank, tt=tt, h=h, slot=slot):
                    ins = None
                    for k in range(KC):
                        ins = e.matmul(bank[:, :], lhsT=hTB[slot][:, k, tt * 128:(tt + 1) * 128],
                                       rhs=Win[:, k, 2 * D + h * 512: 2 * D + (h + 1) * 512],
                                       start=(k == 0), stop=(k == KC - 1))
                    return ins

                P.op("pe", mmv, reads=[Win_b] + hTB_b[slot], writes=[bb])
                evac_copy(vsb[ob][:, tt, h * 512:(h + 1) * 512], bank[:, :], [bb], [vsb_b[ob]])
        P.dma("sp", f"d_qo{ob}", qT_d[j], qsb[ob][:].rearrange("p a b c -> p (a b c)"), reads=[qsb_b[ob]],
              writes=[sbuf_(("q", j))])
        P.dma("sp", f"d_ko{ob}", kT_d[j], ksb[ob][:].rearrange("p a b c -> p (a b c)"), reads=[ksb_b[ob]],
              writes=[sbuf_(("k", j))])
        P.dma("sp", f"d_zo{ob}", szT_d[j], zsb[ob][:].rearrange("p a b c -> p (a b c)"), reads=[zsb_b[ob]],
              writes=[sbuf_(("z", j))])
        P.dma("sp", f"d_vo{ob}", v_d[j * 512:(j + 1) * 512, :].rearrange("(t p) d -> p t d", p=128), vsb[ob][:],
              reads=[vsb_b[ob]], writes=[sbuf_(("v", j))])

    P.barrier()
    if stop_after == "AB_small":
        with nc.Block() as block:
            P.emit(block)
        esB.close()
        esBC.close()
        return
    esB.close()

    esC = ExitStack()

    def sbC(name, shape, dt=F32):
        return esC.enter_context(nc.sbuf_tensor("sbC_" + name, list(shape), dt))

    NH = 16
    mtab = Win[:].rearrange("p k n -> p (k n)")[:, 0:NH * 24 * 64].rearrange("p (h x) -> p h x", h=NH)
    mtab_b = Buf("mtab")
    stgC = [sbC(f"stgC{i}", [128, 2048]) for i in range(2)]
    stgC_b = [Buf(), Buf()]
    load_weight_bf16(Wout, Wout_b, woutn_d, D,
                     [stgC[0][:].rearrange("p (k n) -> p k n", k=KC), stgC[1][:].rearrange("p (k n) -> p k n", k=KC)],
                     stgC_b, 256)
    for i in range(12):
        s_ = i % 2
        P.dma(("sp", "act")[i % 2], f"d_wst{s_}", stgC[s_][:], rpbG_d[:, i * 2048:(i + 1) * 2048],
              writes=[stgC_b[s_]])
        P.op("act", lambda e, s_=s_, i=i: e.activation(
            out=Win[:].rearrange("p k n -> p (k n)")[:, i * 2048:(i + 1) * 2048], in_=stgC[s_][:], func=AF.Exp),
            reads=[stgC_b[s_]], writes=[mtab_b])

    KR = 8
    kring = [sbC(f"kring{i}", [128, KC, 128], BF16) for i in range(KR)]
    vring = [sbC(f"vring{i}", [128, D], BF16) for i in range(KR)]
    kring_b = [Buf() for i in range(KR)]
    vring_b = [Buf() for i in range(KR)]
    qtl = [sbC(f"qtl{i}", [128, KC, 128], BF16) for i in range(2)]
    ztl = [sbC(f"ztl{i}", [128, KC, 128], BF16) for i in range(2)]
    x1t = [sbC(f"x1t{i}", [128, D]) for i in range(2)]
    qtl_b, ztl_b, x1t_b = [Buf(), Buf()], [Buf(), Buf()], [Buf(), Buf()]
    pT = [sbC(f"pT{i}", [128, 2, 896], BF16) for i in range(2)]
    pT_b = [Buf(), Buf()]
    yTC = [sbC(f"yTC{i}", [128, KC, 128], BF16) for i in range(2)]
    yTC_b = [[Buf() for c in range(KC)] for i in range(2)]
    rcC = [sbC(f"rcC{i}", [128, 128]) for i in range(2)]
    wC = [sbC(f"wC{i}", [128, 128]) for i in range(2)]
    rcC_b, wC_b = [Buf(), Buf()], [Buf(), Buf()]
    tmpC = [sbC(f"tmpC{i}", [128, D]) for i in range(2)]
    tmpC_b = [Buf(), Buf()]
    sqjC = sbC("sqjC", [128, 512], BF16)
    ss2C = sbC("ss2C", [128, 8, 2])
    rs2C = sbC("rs2C", [128, 8])
    st2C_b = [Buf() for i in range(8)]
    sS = psS[:].rearrange("p a n -> p (a n)")
    sS_b = [Buf("sA"), Buf("sB")]
    pOs = [pp4[:, 0:256], ptp[:].rearrange("p a b -> p (a b)").bitcast(F32)[:, 0:256]]
    pO_b = [Buf("pO0"), Buf("pO1")]

    NT = L // 128
    ntC = NT if stop_after != "small" else 6

    def rs_of(r):
        return min(max(r - 4, 0), 120)

    def keytiles(t):
        lo = rs_of(2 * t) // 2
        hi = (rs_of(2 * t + 1) + 7) // 2
        return list(range(hi, lo - 1, -1))

    def mask_off(t):
        kts = keytiles(t)
        if len(kts) == 5:
            return 0, 5
        dmax = kts[0] - t
        ti = 6 - 2 * dmax
        return (10 + ti) * 64, 4

    loaded = set()

    def ensure_keys(t):
        for kt in sorted(keytiles(t)):
            if kt in loaded:
                continue
            loaded.add(kt)
            sl = kt % KR
            P.dma("sp", f"d_kr{sl}", kring[sl][:].rearrange("p c n -> p (c n)"),
                  kT_d[kt // 4, :, (kt % 4) * 1024:(kt % 4 + 1) * 1024],
                  reads=[sbuf_(("k", kt // 4))], writes=[kring_b[sl]])
            P.dma("sp", f"d_vr{sl}", vring[sl][:], v_d[kt * 128:(kt + 1) * 128, :],
                  reads=[sbuf_(("v", kt // 4))], writes=[vring_b[sl]])

    def load_tile(t):
        s_ = t % 2
        P.dma("sp", f"d_ql{s_}", qtl[s_][:].rearrange("p c n -> p (c n)"),
              qT_d[t // 4, :, (t % 4) * 1024:(t % 4 + 1) * 1024], reads=[sbuf_(("q", t // 4))], writes=[qtl_b[s_]])
        P.dma("sp", f"d_zl{s_}", ztl[s_][:].rearrange("p c n -> p (c n)"),
              szT_d[t // 4, :, (t % 4) * 1024:(t % 4 + 1) * 1024], reads=[sbuf_(("z", t // 4))], writes=[ztl_b[s_]])
        P.dma("sp", f"d_xl{s_}", x1t[s_][:], x1_d[t * 128:(t + 1) * 128, :], reads=[x1buf(t * 128)],
              writes=[x1t_b[s_]])

    pTh_b = [[Buf(), Buf()], [Buf(), Buf()]]

    def s_exp(t, c, hh):
        ts_ = t % 2
        kts = keytiles(t)
        nk = len(kts)
        ncol = (2 + nk) * 128
        pb = c % 2
        p0 = hh * 64

        def mm(e):
            ins = None
            for s_i in range(2 + nk):
                if s_i < 2:
                    lhs = kcT[p0:p0 + 64, c, s_i * 128:(s_i + 1) * 128]
                else:
                    lhs = kring[kts[s_i - 2] % KR][p0:p0 + 64, c, :]
                ins = e.matmul(sS[:, hh * 1024 + s_i * 128: hh * 1024 + (s_i + 1) * 128], lhsT=lhs,
                               rhs=qtl[ts_][p0:p0 + 64, c, :], start=True, stop=True)
            return ins

        P.op("pe", mm, reads=[kcT_b, qtl_b[ts_]] + [kring_b[kt % KR] for kt in kts], writes=[sS_b[hh]])
        for (a_, b_) in ((0, 512), (512, ncol)):
            P.op("act", lambda e, a_=a_, b_=b_: e.activation(
                out=pT[pb][:, hh, a_:b_], in_=sS[:, hh * 1024 + a_: hh * 1024 + b_], func=AF.Exp),
                reads=[sS_b[hh]], writes=[pTh_b[pb][hh]])

    def maskmul(t, c, hh):
        kts = keytiles(t)
        nk = len(kts)
        ncol = (2 + nk) * 128
        pb = c % 2
        off, nk2 = mask_off(t)
        P.op("dve", lambda e: e.tensor_tensor(
            out=pT[pb][:, hh, 256:ncol], in0=pT[pb][:, hh, 256:ncol],
            in1=mtab[:, 2 * c + hh, off:off + nk * 128], op=ALU.mult),
            reads=[pTh_b[pb][hh], mtab_b], writes=[pTh_b[pb][hh]])

    def pv(t, c, hh):
        kts = keytiles(t)
        nk = len(kts)
        pb = c % 2
        ob = c % 2
        O = pOs[ob][:, 0:128]
        SM = pOs[ob][:, 128:256]
        p0 = hh * 64

        def mm(e):
            ins = None
            n = 2 + nk
            for s_i in range(n):
                if s_i < 2:
                    lhs = vc[:, s_i, c * 128 + p0: c * 128 + p0 + 64]
                else:
                    lhs = vring[kts[s_i - 2] % KR][:, c * 128 + p0: c * 128 + p0 + 64]
                rhs = pT[pb][:, hh, s_i * 128:(s_i + 1) * 128]
                e.matmul(O[p0:p0 + 64, :], lhsT=lhs, rhs=rhs, start=(s_i == 0), stop=(s_i == n - 1),
                         tile_position=(0, p0))
                ins = e.matmul(SM[p0:p0 + 64, :], lhsT=ones64[:, :], rhs=rhs, start=False,
                               stop=(s_i == n - 1), tile_position=(0, p0), skip_group_check=True)
            return ins

        P.op("pe", mm, reads=[vc_b, pTh_b[pb][hh], b_const] + [vring_b[kt % KR] for kt in kts], writes=[pO_b[ob]])

    def norm_items(t, c):
        ts_ = t % 2
        ob = c % 2
        O = pOs[ob][:, 0:128]
        SM = pOs[ob][:, 128:256]

        def r0():
            P.op("dve", lambda e: e.reciprocal(out=rcC[ob][:, 0:64], in_=SM[:, 0:64]), reads=[pO_b[ob]],
                 writes=[rcC_b[ob]])

        def r1():
            P.op("dve", lambda e: e.reciprocal(out=rcC[ob][:, 64:128], in_=SM[:, 64:128]), reads=[pO_b[ob]],
                 writes=[rcC_b[ob]])
            P.op("pool", lambda e: e.tensor_tensor(out=wC[ob][:], in0=rcC[ob][:], in1=ztl[ts_][:, c, :],
                                                   op=ALU.mult),
                 reads=[rcC_b[ob], ztl_b[ts_]], writes=[wC_b[ob]])

        def y_():
            P.op("dve", lambda e: e.tensor_tensor(out=yTC[ts_][:, c, :], in0=O, in1=wC[ob][:], op=ALU.mult),
                 reads=[pO_b[ob], wC_b[ob]], writes=[yTC_b[ts_][c]])
            if c == KC - 1:
                outproj(t)

        return [r0, r1, y_]

    def outproj(t):
        ts_ = t % 2
        si = t % 8
        for h in range(2):
            def mmo(e, h=h):
                ins = None
                for k in range(KC):
                    ins = e.matmul(py[:, h, :], lhsT=yTC[ts_][:, k, :], rhs=Wout[:, k, h * 512:(h + 1) * 512],
                                   start=(k == 0), stop=(k == KC - 1))
                return ins

            P.op("pe", mmo, reads=[Wout_b] + yTC_b[ts_], writes=[py_b[h]])
            P.op("act", lambda e, h=h: e.activation(out=sqjC[:], in_=py[:, h, :], func=AF.Square,
                                                    scale=1.0 / 32.0, accum_out=ss2C[:, si, h:h + 1]),
                 reads=[py_b[h]], writes=[st2C_b[si]])
        P.op("dve", lambda e: e.tensor_tensor(out=ss2C[:, si, 0:1], in0=ss2C[:, si, 0:1], in1=ss2C[:, si, 1:2],
                                              op=ALU.add), reads=[st2C_b[si]], writes=[st2C_b[si]])
        rstd_chain(ss2C[:, si, 0:1], rs2C[:, si:si + 1], st2C_b[si])
        def stt(h):
            P.op("dve", lambda e: e.scalar_tensor_tensor(
                out=tmpC[ts_][:, h * 512:(h + 1) * 512], in0=py[:, h, :], scalar=rs2C[:, si:si + 1],
                in1=GT[:, 2, h * 512:(h + 1) * 512], op0=ALU.mult, op1=ALU.mult),
                reads=[py_b[h], st2C_b[si], GT_b], writes=[tmpC_b[ts_]])

        def fin():
            for h in range(2):
                P.op("dve", lambda e, h=h: e.tensor_tensor(
                    out=tmpC[ts_][:, h * 512:(h + 1) * 512], in0=tmpC[ts_][:, h * 512:(h + 1) * 512],
                    in1=x1t[ts_][:, h * 512:(h + 1) * 512], op=ALU.add),
                    reads=[tmpC_b[ts_], x1t_b[ts_]], writes=[tmpC_b[ts_]])
            P.dma("sp", f"d_oo{ts_}", out_d[t * 128:(t + 1) * 128, :], tmpC[ts_][:], reads=[tmpC_b[ts_]],
                  writes=[dbuf(("out", t))])

        late.append((cur_step[0] + 4, lambda: stt(0)))
        late.append((cur_step[0] + 4, lambda: stt(1)))
        late.append((cur_step[0] + 5, fin))

    late = []
    cur_step = [0]
    items = [(t, c, hh) for t in range(ntC) for c in range(KC) for hh in range(2)]
    deferred = []
    ensure_keys(0)
    load_tile(0)
    LOOK = 2
    for n in range(min(LOOK, len(items))):
        s_exp(*items[n])
    for n, (t, c, hh) in enumerate(items):
        cur_step[0] = n
        if c == 4 and hh == 0 and t + 1 < ntC:
            ensure_keys(t + 1)
            load_tile(t + 1)
        if n + LOOK < len(items):
            s_exp(*items[n + LOOK])
        maskmul(t, c, hh)
        pv(t, c, hh)
        for _ in range(2):
            if deferred:
                deferred.pop(0)()
        if late and late[0][0] <= n:
            late.pop(0)[1]()
        if hh == 1:
            deferred.extend(norm_items(t, c))
    while deferred:
        deferred.pop(0)()
    while late:
        late.pop(0)[1]()
    P.wait_all("sp", [dbuf(("out", t)) for t in range(ntC)])
    P.barrier()
    with nc.Block() as block:
        P.emit(block)
    esC.close()
    esBC.close()


def _rpb_gather(rpb):
    NEG = np.float32(-30000.0)
    out = np.full((128, 16, 24, 64), NEG, dtype=np.float32)
    qc = np.arange(64)
    cs = np.clip(qc - 8, 0, 48)
    for kl in range(2):
        for kc in range(64):
            p = kl * 64 + kc
            valid_q = (cs <= kc) & (kc < cs + 16)
            dc = kc - qc + 15
            qv = qc[valid_q]
            for b in range(24):
                if b < 10:
                    dr = 4 - b + kl
                    if dr < -4 or dr > 3:
                        continue
                else:
                    dr = 6 - (b - 10) + kl
                    if dr < -7 or dr > 7:
                        continue
                out[p, :, b, qv] = rpb[:, dr + 7, dc[valid_q]].T
    return np.ascontiguousarray(out.reshape(128, 16 * 24 * 64))


def prep_inputs(inputs):
    f = lambda a: np.ascontiguousarray(np.asarray(a, dtype=np.float32))
    x, c, ctx, c_ctx = f(inputs["x"]), f(inputs["c"]), f(inputs["ctx"]), f(inputs["c_ctx"])
    g_pre, g_post, w_mod, b_mod = f(inputs["g_pre"]), f(inputs["g_post"]), f(inputs["w_mod"]), f(inputs["b_mod"])
    conv_w = f(inputs["conv_w"])
    shared = {
        "w_mod": w_mod,
        "bmodT": f(b_mod.reshape(2, 24, 128).transpose(2, 0, 1).reshape(128, 48)),
        "bgate": f(b_mod[:, 2 * D:3 * D]),
        "gpreT": f(g_pre.reshape(2, 8, 128).transpose(2, 0, 1).reshape(128, 16)),
        "gpost": g_post,
        "w_in_conv": f(inputs["w_in_conv"][0]),
        "w_out_conv": f(inputs["w_out_conv"][0]),
        "w_in_na": f(inputs["w_in_na"][0]),
        "w_out_na": f(inputs["w_out_na"][0]),
        "convwT": f(conv_w[0].reshape(3, 8, 128).transpose(2, 1, 0).reshape(128, 24)),
        "identd": np.eye(128, dtype=np.float32),
        "rpbG": _rpb_gather(f(inputs["rpb"][0])),
    }
    in_maps = []
    for b in range(NCORES):
        cv2 = np.stack([c[b], c_ctx], 0)
        m = dict(shared)
        m["x"] = x[b]
        m["ctx"] = ctx[b]
        m["cT"] = f(cv2.reshape(2, 8, 128).transpose(2, 0, 1).reshape(128, 16))
        in_maps.append(m)
    return in_maps


def kernel(**inputs):
    nc = build("all")
    in_maps = prep_inputs(inputs)
    res = run_bass_kernel_spmd(nc, in_maps, core_ids=list(range(NCORES)))
    return np.stack([r["out"] for r in res.results], 0)
```
